# Optimizing a Trainium2 kernel written in Bass

```python
import math
import jax, jax.numpy as jnp
from jax import lax
import numpy as np

D_MODEL = 1024
BATCH = 32
SEQ = 256
DEPTH = 2
DEC_BATCH = 8
DEC_SEQ = 1024
PAST_LEN = 512

GRID_W = 64
HEAD_DIM = 64
N_HEADS = 8
N_KV_HEADS = 2
Q_PER_KV = N_HEADS // N_KV_HEADS
ATTN_WIDTH = N_HEADS * HEAD_DIM
KV_WIDTH = N_KV_HEADS * HEAD_DIM
FOURIER_GROUP = 64
N_FOURIER_GROUPS = 8
FOURIER_WIDTH = N_FOURIER_GROUPS * FOURIER_GROUP
EVEN_IN = FOURIER_WIDTH + ATTN_WIDTH + 2 * KV_WIDTH
EVEN_OUT = FOURIER_WIDTH + ATTN_WIDTH
CONV_MIX_WIDTH = D_MODEL
CONV_K = 3
D_FF = 2816
Q_BLOCK = 128
ROPE_THETA = 10000.0
EPS = 1e-6
N_ATTN_LAYERS = (DEPTH + 1) // 2
N_CONV_LAYERS = DEPTH // 2
DEEPNORM_ALPHA = (2 * DEPTH) ** 0.25
DEEPNORM_BETA = (8 * DEPTH) ** -0.25

kernel_name = "hybrid_fourier_gqa_shortconv_dit_step"


def layer_norm(x, g, b):
    xf = x.astype(jnp.float32)
    mu = jnp.mean(xf, axis=-1, keepdims=True)
    xc = xf - mu
    var = jnp.mean(xc * xc, axis=-1, keepdims=True)
    y = xc * lax.rsqrt(var + EPS) * g.astype(jnp.float32) + b.astype(jnp.float32)
    return y.astype(x.dtype)


def rms_norm_head(x, g):
    xf = x.astype(jnp.float32)
    y = xf * lax.rsqrt(jnp.mean(xf * xf, axis=-1, keepdims=True) + EPS) * g.astype(jnp.float32)
    return y.astype(x.dtype)


def conv3_centred(x, w):
    xp = jnp.pad(x, ((0, 0), (1, 1), (0, 0)))
    return xp[:, :-2] * w[0] + xp[:, 1:-1] * w[1] + xp[:, 2:] * w[2]


def axial_angles(n):
    rows = n // GRID_W
    row = jnp.repeat(jnp.arange(rows), GRID_W).astype(jnp.float32)
    col = jnp.tile(jnp.arange(GRID_W), rows).astype(jnp.float32)
    half = HEAD_DIM // 2
    inv = 1.0 / (ROPE_THETA ** (jnp.arange(0, half, 2, dtype=jnp.float32) / half))
    return row[:, None] * inv, col[:, None] * inv


def rotate_half_rope(x, ang):
    cos = jnp.cos(ang)[None, :, None, :].astype(x.dtype)
    sin = jnp.sin(ang)[None, :, None, :].astype(x.dtype)
    x1, x2 = jnp.split(x, 2, axis=-1)
    return jnp.concatenate([x1 * cos - x2 * sin, x1 * sin + x2 * cos], axis=-1)


def apply_axial_rope(x, ang_r, ang_c):
    xr, xc = jnp.split(x, 2, axis=-1)
    return jnp.concatenate([rotate_half_rope(xr, ang_r), rotate_half_rope(xc, ang_c)], axis=-1)


def block_attention(q, k, v):
    b, sq = q.shape[0], q.shape[1]
    nb = sq // Q_BLOCK
    qb = q.reshape(b, nb, Q_BLOCK, N_KV_HEADS, Q_PER_KV, HEAD_DIM).transpose(1, 0, 2, 3, 4, 5)
    kf = k.astype(jnp.float32)
    scale = HEAD_DIM ** -0.5

    def one_block(qblk):
        s = jnp.einsum('bqkgd,bskd->bkgqs', qblk.astype(jnp.float32), kf) * scale
        p = jax.nn.softmax(s, axis=-1).astype(v.dtype)
        return jnp.einsum('bkgqs,bskd->bqkgd', p, v)

    o = lax.map(one_block, qb)
    return o.transpose(1, 0, 2, 3, 4, 5).reshape(b, sq, ATTN_WIDTH)


def fourier_mix(f):
    b, s = f.shape[0], f.shape[1]
    fg = f.reshape(b, s, N_FOURIER_GROUPS, FOURIER_GROUP).astype(jnp.float32)
    out = jnp.fft.fft2(fg, axes=(1, 3), norm='ortho').real
    return out.reshape(b, s, FOURIER_WIDTH).astype(f.dtype)


def even_mixer(h, w_in, q_g, k_g, w_out, ctx_kv):
    b, s = h.shape[0], h.shape[1]
    proj = h @ w_in
    f, q, k, v = jnp.split(proj, [FOURIER_WIDTH, FOURIER_WIDTH + ATTN_WIDTH,
                                  FOURIER_WIDTH + ATTN_WIDTH + KV_WIDTH], axis=-1)
    q = rms_norm_head(q.reshape(b, s, N_HEADS, HEAD_DIM), q_g)
    k = rms_norm_head(k.reshape(b, s, N_KV_HEADS, HEAD_DIM), k_g)
    v = v.reshape(b, s, N_KV_HEADS, HEAD_DIM)
    if ctx_kv is None:
        attn = block_attention(q, k, v)
    else:
        ang_r, ang_c = axial_angles(s)
        q = apply_axial_rope(q, ang_r, ang_c)
        k_lat = apply_axial_rope(k, ang_r, ang_c)
        keys = jnp.concatenate([k_lat, ctx_kv[0].astype(k.dtype)], axis=1)
        vals = jnp.concatenate([v, ctx_kv[1].astype(v.dtype)], axis=1)
        attn = block_attention(q, keys, vals)
    out = jnp.concatenate([fourier_mix(f), attn], axis=-1) @ w_out
    return out, k, v


def conv_mixer(h, w_in, conv_w, w_out):
    bg, cg, xin = jnp.split(h @ w_in, 3, axis=-1)
    return (bg * conv3_centred(cg * xin, conv_w)) @ w_out


def conv_ffn(h, w_up, conv_w, w_down):
    u = conv3_centred(h @ w_up, conv_w)
    a, g = jnp.split(u, 2, axis=-1)
    return (jax.nn.silu(g) * a) @ w_down


def modulation(cond, w, b):
    return jnp.split(jax.nn.silu(cond) @ w + b, 6, axis=-1)


def modulate(x, shift, scale):
    return x * (1.0 + scale[:, None, :]) + shift[:, None, :]


def post_norm_residual(x, out, gate, g, b):
    return layer_norm(DEEPNORM_ALPHA * x + gate[:, None, :] * out, g, b)


def setup_inputs(seed: int = 0) -> dict:
    key = jax.random.key(seed)
    ks = jax.random.split(key, 24)
    f32 = jnp.float32
    nrm = lambda k, shp, s: jax.random.normal(k, shp, f32) * s
    d = D_MODEL
    return {
        "x_prompt": nrm(ks[0], (BATCH, SEQ, d), 1.0),
        "x_sample": nrm(ks[1], (DEC_BATCH, DEC_SEQ, d), 1.0),
        "cache_k": nrm(ks[2], (DEC_BATCH, N_ATTN_LAYERS, PAST_LEN, N_KV_HEADS, HEAD_DIM), 1.0),
        "cache_v": nrm(ks[3], (DEC_BATCH, N_ATTN_LAYERS, PAST_LEN, N_KV_HEADS, HEAD_DIM), 1.0),
        "c": nrm(ks[4], (DEC_BATCH, d), 1.0),
        "c_ctx": nrm(ks[5], (d,), 1.0),
        "w_ada": nrm(ks[6], (DEPTH, d, 6 * d), 0.5 * d ** -0.5),
        "b_ada": nrm(ks[7], (DEPTH, 6 * d), 0.02),
        "ln_g": 1.0 + nrm(ks[8], (DEPTH, 2, d), 0.02),
        "ln_b": nrm(ks[9], (DEPTH, 2, d), 0.02),
        "w_in_a": nrm(ks[10], (N_ATTN_LAYERS, d, EVEN_IN), d ** -0.5),
        "q_norm_g": 1.0 + nrm(ks[11], (N_ATTN_LAYERS, HEAD_DIM), 0.02),
        "k_norm_g": 1.0 + nrm(ks[12], (N_ATTN_LAYERS, HEAD_DIM), 0.02),
        "w_out_a": nrm(ks[13], (N_ATTN_LAYERS, EVEN_OUT, d), DEEPNORM_BETA * EVEN_OUT ** -0.5),
        "w_in_c": nrm(ks[14], (N_CONV_LAYERS, d, 3 * CONV_MIX_WIDTH), d ** -0.5),
        "conv_c": nrm(ks[15], (N_CONV_LAYERS, CONV_K, CONV_MIX_WIDTH), CONV_K ** -0.5),
        "w_out_c": nrm(ks[16], (N_CONV_LAYERS, CONV_MIX_WIDTH, d), DEEPNORM_BETA * CONV_MIX_WIDTH ** -0.5),
        "w_up": nrm(ks[17], (DEPTH, d, 2 * D_FF), d ** -0.5),
        "conv_f": nrm(ks[18], (DEPTH, CONV_K, 2 * D_FF), CONV_K ** -0.5),
        "w_down": nrm(ks[19], (DEPTH, D_FF, d), DEEPNORM_BETA * D_FF ** -0.5),
    }


def reference(x_prompt, x_sample, cache_k, cache_v, c, c_ctx, w_ada, b_ada, ln_g, ln_b,
              w_in_a, q_norm_g, k_norm_g, w_out_a, w_in_c, conv_c, w_out_c,
              w_up, conv_f, w_down):
    xp = x_prompt
    xs = x_sample
    new_k = []
    new_v = []
    for l in range(DEPTH):
        i = l // 2
        sh1_p, sc1_p, g1_p, sh2_p, sc2_p, g2_p = modulation(c_ctx[None, :], w_ada[l], b_ada[l])
        sh1_s, sc1_s, g1_s, sh2_s, sc2_s, g2_s = modulation(c, w_ada[l], b_ada[l])
        hp = modulate(xp, sh1_p, sc1_p)
        hs = modulate(xs, sh1_s, sc1_s)
        if l % 2 == 0:
            op, kp, vp = even_mixer(hp, w_in_a[i], q_norm_g[i], k_norm_g[i], w_out_a[i], None)
            os_, _, _ = even_mixer(hs, w_in_a[i], q_norm_g[i], k_norm_g[i], w_out_a[i],
                                   (cache_k[:, i], cache_v[:, i]))
            new_k.append(kp)
            new_v.append(vp)
        else:
            op = conv_mixer(hp, w_in_c[i], conv_c[i], w_out_c[i])
            os_ = conv_mixer(hs, w_in_c[i], conv_c[i], w_out_c[i])
        xp = post_norm_residual(xp, op, g1_p, ln_g[l, 0], ln_b[l, 0])
        xs = post_norm_residual(xs, os_, g1_s, ln_g[l, 0], ln_b[l, 0])
        fp = conv_ffn(modulate(xp, sh2_p, sc2_p), w_up[l], conv_f[l], w_down[l])
        fs = conv_ffn(modulate(xs, sh2_s, sc2_s), w_up[l], conv_f[l], w_down[l])
        xp = post_norm_residual(xp, fp, g2_p, ln_g[l, 1], ln_b[l, 1])
        xs = post_norm_residual(xs, fs, g2_s, ln_g[l, 1], ln_b[l, 1])
    new_cache_k = jnp.stack(new_k, axis=1)
    new_cache_v = jnp.stack(new_v, axis=1)
    return (xp, xs, new_cache_k, new_cache_v)
```

```python
import math
import numpy as np
import ml_dtypes
import concourse.bass as bass
import concourse.mybir as mybir
from concourse.bass_utils import run_bass_kernel_spmd

F32 = mybir.dt.float32
BF16 = mybir.dt.bfloat16
ALU = mybir.AluOpType
AF = mybir.ActivationFunctionType
AX = mybir.AxisListType

FLAG_POOLTAP = False
FLAG_BG = True
FLAG_DEFER_TR = True
FLAG_BLKOUTER = True

NCORES = 8
D = 1024
T = 1024
DFF = 2816
NHC = 22
ALPHA = 4.0 ** 0.25
EPS = 1e-6
EPS_LN = EPS / (ALPHA * ALPHA)
COMPUTE = ("pe", "act", "dve", "pool")


class Tile:
    __slots__ = ("name", "last_w", "readers", "excl")

    def __init__(self, name="", excl=False):
        self.name = name
        self.last_w = None
        self.readers = []
        self.excl = excl


class Op:
    __slots__ = ("eng", "fn", "deps", "is_dma", "sem", "semval", "milestone", "mcount", "idx")

    def __init__(self, eng, fn, is_dma):
        self.idx = 0
        self.eng = eng
        self.fn = fn
        self.deps = []
        self.is_dma = is_dma
        self.sem = None
        self.semval = 0
        self.milestone = False
        self.mcount = 0


class Sched:
    def __init__(self, nc, n_dma_sems=24):
        self.nc = nc
        self.ops = {e: [] for e in ("pe", "act", "dve", "pool", "sp")}
        self.esem = {e: nc.alloc_semaphore("sem_" + e) for e in COMPUTE}
        self.dsems = [nc.alloc_semaphore("dsem%d" % i) for i in range(n_dma_sems)]
        self.dsem_last = [None] * n_dma_sems
        self.dsem_uses = [0] * n_dma_sems
        half = n_dma_sems // 2
        self.dq = {"pool": [list(range(0, half)), 0], "sp": [list(range(half, n_dma_sems)), 0]}
        self.out_dmas = []
        self.nops = 0

    def _add(self, eng, fn, reads, writes, is_dma):
        op = Op(eng, fn, is_dma)
        self.nops += 1
        op.idx = self.nops
        deps = {}
        for t in reads:
            if t.last_w is not None:
                deps[id(t.last_w)] = t.last_w
            if t.excl:
                for r in t.readers:
                    if r.is_dma or r.eng != eng:
                        deps[id(r)] = r
        for t in writes:
            if t.last_w is not None:
                deps[id(t.last_w)] = t.last_w
            for r in t.readers:
                deps[id(r)] = r
        best = {}
        for d in deps.values():
            if d is op:
                continue
            if (not is_dma) and (not d.is_dma) and d.eng == "pe" and eng == "pe":
                continue
            if d.is_dma:
                op.deps.append(d)
            else:
                b = best.get(d.eng)
                if b is None or d.idx > b.idx:
                    best[d.eng] = d
        for d in best.values():
            d.milestone = True
            op.deps.append(d)
        if is_dma:
            q = self.dq[eng]
            i = q[0][q[1] % len(q[0])]
            q[1] += 1
            prev = self.dsem_last[i]
            if prev is not None:
                op.deps.append(prev)
            self.dsem_uses[i] += 1
            op.sem = self.dsems[i]
            op.semval = 16 * self.dsem_uses[i]
            self.dsem_last[i] = op
        for t in reads:
            if not is_dma:
                t.readers = [r for r in t.readers if r.is_dma or r.eng != eng]
            t.readers.append(op)
        for t in writes:
            t.last_w = op
            t.readers = []
        self.ops[eng].append(op)
        return op

    def op(self, eng, fn, reads=(), writes=()):
        return self._add(eng, fn, reads, writes, False)

    def dma(self, eng, fn, reads=(), writes=(), is_output=False):
        o = self._add(eng, fn, reads, writes, True)
        if is_output:
            self.out_dmas.append(o)
        return o

    def finish(self):
        nc = self.nc
        for e in COMPUTE:
            if self.ops[e]:
                self.ops[e][-1].milestone = True
        final_deps = list(self.out_dmas) + [self.ops[e][-1] for e in COMPUTE
                                            if self.ops[e] and not self.ops[e][-1].is_dma]
        for d in self.dsem_last:
            if d is not None:
                final_deps.append(d)
        fin = Op("sp", None, False)
        fin.deps = final_deps
        self.ops["sp"].append(fin)
        for e in COMPUTE:
            c = 0
            for o in self.ops[e]:
                if o.milestone and not o.is_dma:
                    c += 1
                    o.mcount = c
        esem = self.esem

        def emit_engine(ename, eng):
            seen = {}
            for o in self.ops[ename]:
                waits = {}
                for d in o.deps:
                    if d.is_dma:
                        key = ("d", id(d.sem))
                        sem, val = d.sem, d.semval
                    else:
                        key = ("e", d.eng)
                        sem, val = esem[d.eng], d.mcount
                    if seen.get(key, 0) >= val:
                        continue
                    if key not in waits or waits[key][1] < val:
                        waits[key] = (sem, val)
                for key, (sem, val) in waits.items():
                    eng.wait_ge(sem, val)
                    seen[key] = val
                if o.fn is None:
                    continue
                ins = o.fn(eng)
                if o.is_dma:
                    ins.then_inc(o.sem, 16)
                elif o.milestone:
                    ins.then_inc(esem[ename], 1)

        with nc.Block() as block:
            @block.tensor
            def _(e):
                emit_engine("pe", e)

            @block.scalar
            def _(e):
                emit_engine("act", e)

            @block.vector
            def _(e):
                emit_engine("dve", e)

            @block.gpsimd
            def _(e):
                emit_engine("pool", e)

            @block.sync
            def _(e):
                emit_engine("sp", e)


class Rot:
    def __init__(self, items):
        self.items = list(items)
        self.i = 0

    def next(self):
        x = self.items[self.i % len(self.items)]
        self.i += 1
        return x


def _vec_layout():
    off = {}
    o = 0
    for name, n in (("bada", 96), ("lng", 32), ("lnb", 32), ("convf", 264), ("convc", 24),
                    ("cond", 16), ("gqk", 128), ("cosF", 512), ("sinF", 512)):
        off[name] = (o, n)
        o += n
    return off, o


VOFF, NVEC = _vec_layout()


def build_program():
    nc = bass.Bass("TRN2", target_bir_lowering=False)
    S = Sched(nc)

    def din(name, shape, dt=F32):
        return nc.dram_tensor(name, list(shape), dt, kind="ExternalInput").ap()

    def dout(name, shape, dt=F32):
        return nc.dram_tensor(name, list(shape), dt, kind="ExternalOutput").ap()

    x_in = din("xT", [128, 8, 2 * T])
    vecs_in = din("vecs", [128, NVEC])
    cbf_in = din("cbf", [128, 512], BF16)
    ck_in = din("ckT", [128, 2, 512])
    cv_in = din("cv", [128, 4, 128])
    dftp_in = din("dftp", [2, 256, 256])
    dfts_in = din("dfts", [2, 1024, 1024])
    w_ada = din("w_ada", [2, D, 6 * D])
    w_in_a = din("w_in_a", [1, D, 1280])
    w_out_a = din("w_out_a", [1, D, D])
    w_in_c = din("w_in_c", [1, D, 3 * D])
    w_out_c = din("w_out_c", [1, D, D])
    w_up = din("w_up", [2, D, 2 * DFF])
    w_down = din("w_down", [2, DFF, D])
    y_out = dout("yT", [128, 8, 2 * T])
    nk_out = dout("nk", [T, 128])
    nv_out = dout("nv", [T, 128])

    sb = nc.alloc_sbuf_tensor
    xT = sb("xTs", [128, 8, T], F32)
    tx = [[Tile("x%d_%d" % (c, b)) for b in range(2)] for c in range(8)]
    actA = sb("actA", [128, 8, T], BF16)
    tA = [[Tile("A%d_%d" % (c, b)) for b in range(2)] for c in range(8)]
    NSLOT = 6
    slots = [sb("slot%d" % i, [128, 8, 512], BF16) for i in range(NSLOT)]
    tslot = [[Tile("s%d_lo" % i), Tile("s%d_hi" % i)] for i in range(NSLOT)]
    slot_rot = Rot(range(NSLOT))
    vec = sb("vec", [128, NVEC], F32)
    tvec = Tile("vec")
    cbf = sb("cbfs", [128, 512], BF16)
    tcbf = Tile("cbf")
    identb = cbf[:, 0:128]
    onesD = cbf[:, 128:256]
    BDc = cbf[:, 256:384]
    BDsn = cbf[:, 384:512]
    onesf = sb("onesf", [128, 128], F32)
    tonesf = Tile("onesf")
    mod_sb = sb("mod_sb", [128, 2, 96], F32)
    tmod = [Tile("mod0"), Tile("mod1")]
    NDV = 16
    dv = sb("dv", [128, 2, 2, NDV, 8], F32)
    tdv = [[Tile("dv%d%d" % (l, c)) for c in range(2)] for l in range(2)]
    scT = sb("scT", [128, 16], BF16)
    tscT = Tile("scT")
    gqk10 = sb("gqk10", [128, 10, 64], F32)
    tg10 = Tile("gqk10")
    epsln = sb("epsln", [128, 1], F32)
    tepsln = Tile("epsln")
    negb = sb("negb", [128, 1], F32)
    tnegb = Tile("negb")
    tmp_small = sb("tmp_small", [128, 256], F32)
    ttmp_small = Tile("tmp_small")
    tedge = [Tile("edge%d" % i) for i in range(32)]
    tseam = [Tile("seam0"), Tile("seam1")]
    m2_sb = sb("m2_sb", [128, 512], F32)
    rstd_sb = sb("rstd_sb", [128, 512], F32)
    nmr_sb = sb("nmr_sb", [128, 512], F32)
    tm2, trstd, tnmr = Tile("m2"), Tile("rstd"), Tile("nmr")
    zb = [sb("zb%d" % i, [128, 512], BF16) for i in range(2)]
    zsq = [sb("zsq%d" % i, [128, 512], BF16) for i in range(2)]
    tzb = [Tile("zb%d" % i) for i in range(2)]
    tzsq = [Tile("zsq%d" % i) for i in range(2)]
    zrot = Rot(range(2))
    ntmp = [sb("ntmp%d" % i, [128, 512], F32) for i in range(2)]
    tntmp = [Tile("ntmp%d" % i) for i in range(2)]
    nrot = Rot(range(2))

    UW = 1032
    RR_N = 41824
    RR = sb("RR", [128, RR_N], BF16)
    actT = RR[:, 0:NHC * T].rearrange("p (c t) -> p c t", t=T)
    tact = [[Tile("act%d_%d" % (c, b)) for b in range(2)] for c in range(NHC)]
    cb0 = NHC * T
    convbuf = [RR[:, cb0 + i * UW * 2: cb0 + (i + 1) * UW * 2].bitcast(F32) for i in range(8)]
    tconv2 = [[Tile("cv%d_0" % i), Tile("cv%d_1" % i)] for i in range(8)]
    tconv = [None] * 8
    o = 0

    def carve(n_bf16):
        nonlocal o
        a = RR[:, o:o + n_bf16]
        o += n_bf16
        return a
    fS = carve(8 * 512).rearrange("p (i n) -> p i n", n=512)
    tfS = [Tile("fS%d" % i) for i in range(8)]
    qT = carve(4 * T).rearrange("p (j t) -> p j t", t=T)
    tqT = [Tile("qT%d" % i) for i in range(8)]
    KT = 1536
    kT = carve(4 * KT).rearrange("p (k v t) -> p k v t", v=2, t=KT)
    tkz = Tile("kTzero")
    tkT = [Tile("kT%d" % i) for i in range(12)]
    vaug = carve(12 * 2 * 192).rearrange("p (i k n) -> p i k n", k=2, n=192)
    tva = [Tile("va%d" % i) for i in range(12)]
    Gb = carve(2 * 4 * T).rearrange("p (s c t) -> p s c t", s=2, c=4)
    tG = [[[Tile("G%d%d%d" % (s, c, b)) for b in range(2)] for c in range(4)] for s in range(2)]
    pT2 = [carve(1024) for _ in range(2)]
    tpT2 = [Tile("pT2_%d" % i) for i in range(2)]
    sqbuf = carve(640 * 2).bitcast(F32)
    tsq = Tile("sqbuf")
    qn = [carve(640 * 2).bitcast(F32).rearrange("p (h d) -> p h d", d=64) for _ in range(2)]
    tqn = [Tile("qn0"), Tile("qn1")]
    rt1 = carve(640 * 2).bitcast(F32).rearrange("p (h d) -> p h d", d=64)
    rt2 = carve(640 * 2).bitcast(F32).rearrange("p (h d) -> p h d", d=64)
    trt1, trt2 = Tile("rt1"), Tile("rt2")
    q16k = [carve(6 * 128).rearrange("p (m n) -> p m n", n=128) for _ in range(2)]
    tq16 = [Tile("q16_0"), Tile("q16_1")]
    vst = [carve(128 * 2).bitcast(F32) for _ in range(2)]
    tvst = [Tile("vst0"), Tile("vst1")]
    msb = carve(16 * 2 * 3).bitcast(F32)
    tms = Tile("ms")
    lnrow = [carve(512 * 2).bitcast(F32) for _ in range(2)]
    tlnrow = [Tile("lnrow0"), Tile("lnrow1")]
    recb = [carve(512 * 2).bitcast(F32) for _ in range(2)]
    trecb = [Tile("recb0"), Tile("recb1")]
    assert o <= RR_N, (o, RR_N)
    mix_tiles = (tfS + tqT + tkT + [tkz] + tva + [t for a in tG for b_ in a for t in b_] + tpT2 + [tsq] + tqn +
                 [trt1, trt2] + tq16 + tvst + [tms] + tlnrow + trecb)
    ffn_tiles = [t for a in tact for t in a] + [t for a in tconv2 for t in a]

    def handoff(old, new):
        pend = {}
        for t in old:
            if t.last_w is not None:
                pend[id(t.last_w)] = t.last_w
            for r in t.readers:
                pend[id(r)] = r
        pl = list(pend.values())
        for t in new:
            t.readers = list(t.readers) + pl

    PSall = nc.alloc_psum_tensor("psall", [128, 8 * 512], F32)
    PS = [PSall[:, i * 512:(i + 1) * 512] for i in range(8)]
    tps = [Tile("ps%d" % i, excl=True) for i in range(8)]

    def MM(ps_ap, lhsT, rhs, start, stop, reads, writes):
        S.op("pe", lambda e: e.matmul(ps_ap, lhsT, rhs, start=start, stop=stop), reads, writes)

    def TR(ps_ap, in_ap, reads, writes):
        S.op("pe", lambda e: e.transpose(ps_ap, in_ap, identb), list(reads) + [tcbf], writes)

    def ACT(out, in_, func, reads, writes, scale=1.0, bias=0.0):
        S.op("act", lambda e: e.activation(out=out, in_=in_, func=func, scale=scale, bias=bias), reads, writes)

    def TT(out, in0, in1, op, reads, writes, eng="dve"):
        S.op(eng, lambda e: e.tensor_tensor(out=out, in0=in0, in1=in1, op=op), reads, writes)

    def STT(out, in0, scalar, in1, op0, op1, reads, writes):
        S.op("dve", lambda e: e.scalar_tensor_tensor(out=out, in0=in0, scalar=scalar, in1=in1, op0=op0, op1=op1),
             reads, writes)

    def TS(out, in0, s1, s2, op0, op1, reads, writes, eng="dve"):
        S.op(eng, lambda e: e.tensor_scalar(out=out, in0=in0, scalar1=s1, scalar2=s2, op0=op0, op1=op1), reads, writes)

    def CP(out, in_, reads, writes, eng="dve"):
        S.op(eng, lambda e: e.tensor_copy(out=out, in_=in_), reads, writes)

    def MS(ap, val, writes, eng="dve"):
        S.op(eng, lambda e: e.memset(ap, val), (), writes)

    def LD(out, in_, writes, q="sp"):
        S.dma(q, lambda e: e.dma_start(out=out, in_=in_), (), writes)

    def ST(out, in_, reads):
        S.dma("sp", lambda e: e.dma_start(out=out, in_=in_), reads, (), is_output=True)

    pinned = set()

    def wload(parts, pin=False):
        i = slot_rot.next()
        while i in pinned:
            i = slot_rot.next()
        if pin:
            pinned.add(i)
        for (view, nk, c0, ncols) in parts:
            tl = [tslot[i][0]] if c0 + ncols <= 256 else ([tslot[i][1]] if c0 >= 256 else tslot[i])
            src = view.rearrange("(k p) n -> p k n", p=128)
            dst = slots[i][:, 0:nk, c0:c0 + ncols]
            S.dma("pool", lambda e, dst=dst, src=src: e.dma_start(out=dst, in_=src), (), tl)
        return slots[i], tslot[i]

    def unpin(slot):
        for i_, sl_ in enumerate(slots):
            if sl_ is slot:
                pinned.discard(i_)

    def vslice(name, a=0, n=None):
        o0, nn = VOFF[name]
        n = nn - a if n is None else n
        return vec[:, o0 + a:o0 + a + n]

    def dvs(l, cnd, idx, c):
        return dv[:, l, cnd, idx, c:c + 1]

    A1S, A1B, G1, G2, LGA, LBA, LGB, LBB, LG1, LB1, LG2, LB2, TMPV = range(13)

    LD(vec[:, :], vecs_in, [tvec])
    LD(cbf[:, :], cbf_in, [tcbf])
    MS(onesf[:, :], 1.0, [tonesf])
    MS(epsln[:, :], EPS_LN, [tepsln])
    ACT(scT[:, :], vslice("cond"), AF.Silu, [tvec], [tscT])
    gq_b = vslice("gqk", 0, 64).unsqueeze(1).broadcast_to([128, 8, 64])
    gk_b = vslice("gqk", 64, 64).unsqueeze(1).broadcast_to([128, 2, 64])
    TS(gqk10[:, 0:8, :], gq_b, 0.125, None, ALU.mult, ALU.bypass, [tvec], [tg10])
    CP(gqk10[:, 8:10, :], gk_b, [tvec], [tg10])
    TS(tmp_small[:, 0:128], vslice("gqk"), -1.0, None, ALU.mult, ALU.bypass, [tvec], [ttmp_small])
    TT(tmp_small[:, 0:128], tmp_small[:, 0:128], vslice("gqk"), ALU.max, [tvec, ttmp_small], [ttmp_small])
    S.op("dve", lambda e: e.tensor_reduce(out=tmp_small[:, 128:130],
                                          in_=tmp_small[:, 0:128].rearrange("p (a b) -> p a b", b=64),
                                          op=ALU.max, axis=AX.X), [ttmp_small], [ttmp_small])
    STT(negb[:, :], tmp_small[:, 128:129], -8.0, tmp_small[:, 129:130], ALU.mult, ALU.mult, [ttmp_small], [tnegb])

    bg_jobs = [(l_, s_) for l_ in range(2) for s_ in range(12)]
    bg_state = {"loaded": [], "next": 0, "done": 0}
    BG_DEPTH = 2

    def bg_issue(force=False):
        if not (FLAG_BG or force):
            return
        while len(bg_state["loaded"]) < (BG_DEPTH if FLAG_BG else 1) and bg_state["next"] < len(bg_jobs):
            l, sj = bg_jobs[bg_state["next"]]
            bg_state["next"] += 1
            slot, tl = wload([(w_ada[l][:, sj * 512:(sj + 1) * 512], 8, 0, 512)], pin=True)
            bg_state["loaded"].append((l, sj, slot, tl))

    def bg_compute(bank=None):
        if not bg_state["loaded"]:
            return
        l, sj, slot, tl = bg_state["loaded"].pop(0)
        pi = mainrot.next() if bank is None else bank
        psm = PS[pi]
        for jj in range(4):
            for k in range(8):
                MM(psm[:, 2 * jj:2 * jj + 2], slot[:, k, jj * 128:(jj + 1) * 128], scT[:, 2 * k:2 * k + 2],
                   k == 0, k == 7, tl + [tscT], [tps[pi]])
        for i_, sl_ in enumerate(slots):
            if sl_ is slot:
                pinned.discard(i_)
        j0 = 4 * sj
        bb = vslice("bada", l * 48 + j0, 4).unsqueeze(2).broadcast_to([128, 4, 2])
        TT(mod_sb[:, l, 2 * j0:2 * j0 + 8].rearrange("p (j c) -> p j c", c=2),
           psm[:, 0:8].rearrange("p (j c) -> p j c", c=2), bb, ALU.add, [tps[pi], tvec], [tmod[l]])
        bg_state["done"] += 1

    def bg_step(bank=None):
        if not FLAG_BG:
            return
        bg_compute(bank)
        bg_issue()

    def bg_until(n):
        while bg_state["done"] < n:
            bg_issue(force=True)
            bg_compute()

    def derive_in(l):
        for cnd in range(2):
            m = mod_sb[:, l, :].rearrange("p (j c) -> p j c", c=2)
            d = dv[:, l, cnd]
            rd = [tmod[l], tdv[l][cnd]]
            wr = [tdv[l][cnd]]
            TS(d[:, A1S, :], m[:, 8:16, cnd], 1.0, None, ALU.add, ALU.bypass, rd, wr)
            CP(d[:, A1B, :], m[:, 0:8, cnd], rd, wr)

    def derive(l):
        for cnd in range(2):
            m = mod_sb[:, l, :].rearrange("p (j c) -> p j c", c=2)

            def mv(k):
                return m[:, 8 * k:8 * k + 8, cnd]
            d = dv[:, l, cnd]
            rd = [tmod[l], tvec, tdv[l][cnd]]
            wr = [tdv[l][cnd]]
            lg1 = vslice("lng", (l * 2 + 0) * 8, 8)
            lb1 = vslice("lnb", (l * 2 + 0) * 8, 8)
            lg2 = vslice("lng", (l * 2 + 1) * 8, 8)
            lb2 = vslice("lnb", (l * 2 + 1) * 8, 8)
            TS(d[:, G1, :], mv(2), 1.0 / ALPHA, None, ALU.mult, ALU.bypass, rd, wr)
            TS(d[:, G2, :], mv(5), 1.0 / ALPHA, None, ALU.mult, ALU.bypass, rd, wr)
            TS(d[:, TMPV, :], mv(4), 1.0, None, ALU.add, ALU.bypass, rd, wr)
            TT(d[:, LGA, :], lg1, d[:, TMPV, :], ALU.mult, rd, wr)
            TT(d[:, LBA, :], lb1, d[:, TMPV, :], ALU.mult, rd, wr)
            TT(d[:, LBA, :], d[:, LBA, :], mv(3), ALU.add, rd, wr)
            CP(d[:, LG1, :], lg1, rd, wr)
            CP(d[:, LB1, :], lb1, rd, wr)
            CP(d[:, LG2, :], lg2, rd, wr)
            CP(d[:, LB2, :], lb2, rd, wr)

    def derive_cross():
        for cnd in range(2):
            d0 = dv[:, 0, cnd]
            d1 = dv[:, 1, cnd]
            rd = [tdv[0][cnd], tdv[1][cnd]]
            wr = [tdv[0][cnd]]
            TT(d0[:, LGB, :], d0[:, LG2, :], d1[:, A1S, :], ALU.mult, rd, wr)
            TT(d0[:, LBB, :], d0[:, LB2, :], d1[:, A1S, :], ALU.mult, rd, wr)
            TT(d0[:, LBB, :], d0[:, LBB, :], d1[:, A1B, :], ALU.add, rd, wr)

    mainrot = Rot([0, 1, 2, 3])
    rot8 = Rot(range(8))

    class LNState:
        pass

    def ln_begin():
        st = LNState()
        st.cnt = [0, 0]
        st.pending = []
        return st

    def ln_accum(st, l, cnd, gate_idx, d, blk, ps_i, pool_sq=False):
        xs = xT[:, d, blk * 512:(blk + 1) * 512]
        STT(xs, PS[ps_i][:, :], dvs(l, cnd, gate_idx, d), xs, ALU.mult, ALU.add,
            [tps[ps_i], tx[d][blk], tdv[l][cnd]], [tx[d][blk]])
        zi = zrot.next()
        ACT(zb[zi][:, :], xs, AF.Copy, [tx[d][blk]], [tzb[zi]])
        if pool_sq:
            TT(zsq[zi][:, :], xs, xs, ALU.mult, [tx[d][blk]], [tzsq[zi]], eng="pool")
        else:
            ACT(zsq[zi][:, :], xs, AF.Square, [tx[d][blk]], [tzsq[zi]])
        st.pending.append((blk, zi))
        if len(st.pending) > 1:
            ln_flush_one(st)

    def ln_flush_one(st):
        blk, zi = st.pending.pop(0)
        first = st.cnt[blk] == 0
        last = st.cnt[blk] == 7
        st.cnt[blk] += 1
        MM(PS[4 + 2 * blk][:, :], onesD, zb[zi][:, :], first, last, [tzb[zi], tcbf], [tps[4 + 2 * blk]])
        MM(PS[5 + 2 * blk][:, :], onesD, zsq[zi][:, :], first, last, [tzsq[zi], tcbf], [tps[5 + 2 * blk]])

    def ln_fin_stats(st, blk):
        psum_, pssq_ = 4 + 2 * blk, 5 + 2 * blk
        ACT(m2_sb[:, :], PS[psum_][:, :], AF.Square, [tps[psum_]], [tm2])
        STT(m2_sb[:, :], m2_sb[:, :], -1.0, PS[pssq_][:, :], ALU.mult, ALU.add, [tps[pssq_], tm2], [tm2])
        ACT(rstd_sb[:, :], m2_sb[:, :], AF.Ln, [tm2, tepsln], [trstd], bias=epsln[:, 0:1])
        ACT(rstd_sb[:, :], rstd_sb[:, :], AF.Exp, [trstd], [trstd], scale=-0.5)
        STT(nmr_sb[:, :], PS[psum_][:, :], -1.0, rstd_sb[:, :], ALU.mult, ALU.mult, [tps[psum_], trstd], [tnmr])

    def ln_fin_chunk(st, l, cnd, blk, d, g_idx, b_idx, hg_idx, hb_idx):
        xs = xT[:, d, blk * 512:(blk + 1) * 512]
        ni = nrot.next()
        TT(ntmp[ni][:, :], xs, rstd_sb[:, :], ALU.mult, [tx[d][blk], trstd], [tntmp[ni]])
        TT(ntmp[ni][:, :], ntmp[ni][:, :], nmr_sb[:, :], ALU.add, [tntmp[ni], tnmr], [tntmp[ni]])
        ACT(xs, ntmp[ni][:, :], AF.Identity, [tntmp[ni], tdv[l][cnd]], [tx[d][blk]],
            scale=dvs(l, cnd, g_idx, d), bias=dvs(l, cnd, b_idx, d))
        if hg_idx is not None:
            ACT(actA[:, d, blk * 512:(blk + 1) * 512], ntmp[ni][:, :], AF.Identity, [tntmp[ni], tdv[l][cnd]],
                [tA[d][blk]], scale=dvs(l, cnd, hg_idx, d), bias=dvs(l, cnd, hb_idx, d))

    def ln_finalize(st, l, cnd, blk, g_idx, b_idx, hg_idx, hb_idx, flush=True):
        while flush and st.pending:
            ln_flush_one(st)
        ln_fin_stats(st, blk)
        for d in range(8):
            ln_fin_chunk(st, l, cnd, blk, d, g_idx, b_idx, hg_idx, hb_idx)

    prefetched_up = {}
    wdown_pref = {}

    def up_slot_parts(l, s_):
        return [(w_up[l][:, s_ * 256:(s_ + 1) * 256], 8, 0, 256),
                (w_up[l][:, DFF + s_ * 256:DFF + (s_ + 1) * 256], 8, 256, 256)]

    def outproj_ln(l, cnd, wd, nK, rhs_fn, gate_idx, g_idx, b_idx, hg_idx, hb_idx):
        st = ln_begin()
        while bg_state["loaded"]:
            bg_compute()
        if nK == 8:
            prefetched_up[(l, cnd)] = {s_: wload(up_slot_parts(l, s_), pin=True) for s_ in range(2)}
        kgroups = [(k0, min(8, nK - k0)) for k0 in range(0, nK, 8)]
        sl = {}
        pre = wdown_pref.pop((l, cnd), []) if nK == NHC else []
        for ch in range(2):
            sl[ch] = []
            for gi, (k0, nk) in enumerate(kgroups):
                if ch == 0 and gi < len(pre):
                    sl[ch].append(pre[gi])
                else:
                    sl[ch].append(wload([(wd[k0 * 128:(k0 + nk) * 128, ch * 512:(ch + 1) * 512], nk, 0, 512)], pin=True))
        for ch in range(2):
            for w_ in sl[ch]:
                unpin(w_[0])
        fin = [False, False]
        fin_next = 8
        if not FLAG_BLKOUTER:
            order = [(blk, ch * 4 + dc) for ch in range(2) for blk in range(2) for dc in range(4)]
        else:
            order = [(blk, d) for blk in range(2) for d in range(8)]
        for (blk, d) in order:
            if True:
                ch, dc = d // 4, d % 4
                pi = mainrot.next()
                for kc in range(nK):
                    slot, tl = sl[ch][kc // 8]
                    rap, rt = rhs_fn(kc, blk)
                    MM(PS[pi][:, :], slot[:, kc % 8, dc * 128:(dc + 1) * 128], rap, kc == 0, kc == nK - 1,
                       tl + [rt], [tps[pi]])
                ln_accum(st, l, cnd, gate_idx, d, blk, pi, pool_sq=(nK == 8))
                if FLAG_BLKOUTER and blk == 1 and st.cnt[0] == 8 and not fin[0]:
                    fin[0] = True
                    ln_fin_stats(st, 0)
                    fin_next = 0
                elif FLAG_BLKOUTER and blk == 1 and fin[0] and fin_next < 8:
                    for _ in range(2):
                        ln_fin_chunk(st, l, cnd, 0, fin_next, g_idx, b_idx, hg_idx, hb_idx)
                        fin_next += 1
        if not fin[0]:
            ln_finalize(st, l, cnd, 0, g_idx, b_idx, hg_idx, hb_idx)
        else:
            while fin_next < 8:
                ln_fin_chunk(st, l, cnd, 0, fin_next, g_idx, b_idx, hg_idx, hb_idx)
                fin_next += 1
        ln_finalize(st, l, cnd, 1, g_idx, b_idx, hg_idx, hb_idx)

    def seg_view(buf, h, blk):
        if h == 0:
            c0 = 1 + 258 * (2 * blk)
            return buf[:, c0:c0 + 2 * 258 - 1].rearrange("p (s w) -> p s w", w=258)[:, :, 0:256] if False else \
                buf[:, 258 * 2 * blk:258 * 2 * (blk + 1)].rearrange("p (s w) -> p s w", w=258)[:, :, 1:257]
        return buf[:, 1 + 512 * blk:1 + 512 * (blk + 1)]

    def ps_view(ap, h):
        if h == 0:
            return ap.rearrange("p (s w) -> p s w", w=256)
        return ap

    def tok_view(buf, h):
        if h == 0:
            return buf[:, 0:4 * 258].rearrange("p (s w) -> p s w", w=258)[:, :, 1:257]
        return buf[:, 1:1 + T]

    def tok_flat(ap, h):
        if h == 0:
            return ap.rearrange("p (s w) -> p s w", w=256)
        return ap

    def conv_width(h):
        return 4 * 258 - 2 if h == 0 else T

    def conv3(ubuf, tu, ybuf, ty, w0, w1, w2, wt, h):
        W = conv_width(h)
        ACT(ybuf[:, 1:1 + W], ubuf[:, 1:1 + W], AF.Identity, tu + [wt], ty, scale=w1)
        STT(ybuf[:, 1:1 + W], ubuf[:, 0:W], w0, ybuf[:, 1:1 + W], ALU.mult, ALU.add, tu + ty + [wt], ty)
        STT(ybuf[:, 1:1 + W], ubuf[:, 2:2 + W], w2, ybuf[:, 1:1 + W], ALU.mult, ALU.add, tu + ty + [wt], ty)

    def load_x(h):
        for c in range(8):
            S.dma("sp", lambda e, c=c: e.dma_start(out=xT[:, c, :], in_=x_in[:, c, h * T:(h + 1) * T]),
                  (), [tx[c][0], tx[c][1]])

    def store_y(h):
        for c in range(8):
            S.dma("sp", lambda e, c=c: e.dma_start(out=y_out[:, c, h * T:(h + 1) * T], in_=xT[:, c, :]),
                  [tx[c][0], tx[c][1]], (), is_output=True)

    def modulate_in(l, cnd):
        for c in range(8):
            for blk in range(2):
                ACT(actA[:, c, blk * 512:(blk + 1) * 512], xT[:, c, blk * 512:(blk + 1) * 512], AF.Identity,
                    [tx[c][blk], tdv[l][cnd]], [tA[c][blk]], scale=dvs(l, cnd, A1S, c), bias=dvs(l, cnd, A1B, c))

    def mixer_even(h):
        cnd = h
        MS(vaug.rearrange("p i k n -> p (i k n)").bitcast(F32), 0.0, tva)
        MS(vaug.rearrange("p i k n -> p (i k) n")[:, :, 64:65], 1.0, tva)
        nkt = 8 if h == 0 else 12
        kflat = kT.rearrange("p k v t -> p (k v t)").bitcast(F32)
        MS(kflat, 0.0, [tkz] + tkT)
        wf = wload([(w_in_a[0][:, 0:512], 8, 0, 512)], pin=True)
        wq = wload([(w_in_a[0][:, 512:1024], 8, 0, 512)], pin=True)
        wkv = wload([(w_in_a[0][:, 1024:1280], 8, 0, 256)], pin=True)
        if h == 0:
            bg_issue()
        if h == 1:
            S.dma("pool", lambda e: e.dma_start(out=kT[0:64, :, 0, 1024:1536], in_=ck_in[0:64]), (), tkT[8:12])
            S.dma("pool", lambda e: e.dma_start(out=kT[64:128, :, 1, 1024:1536], in_=ck_in[64:128]), (), tkT[8:12])
            cvv = cv_in.rearrange("p t (k d) -> p t k d", d=64)
            S.dma("pool", lambda e: e.dma_start(out=vaug[:, 8:12, :, 0:64], in_=cvv), (), tva[8:12])
            S.dma("pool", lambda e: e.dma_start(out=vaug[:, 8:12, :, 128:192], in_=cvv), (), tva[8:12])
        prot3 = Rot([(0, 1, 2), (3, 4, 5)])
        trot = Rot([6, 7])
        pend_tr = []
        for i in range(8):
            blk = i // 4
            pf, pq, pkv = prot3.next()
            for (pi, (slot, tl), ncol) in ((pf, wf, 512), (pq, wq, 512), (pkv, wkv, 256)):
                for k in range(8):
                    MM(PS[pi][:, 0:ncol], actA[:, k, i * 128:(i + 1) * 128], slot[:, k, 0:ncol], k == 0, k == 7,
                       tl + [tA[k][blk]], [tps[pi]])
            while pend_tr:
                pend_tr.pop(0)()
            b2 = i % 2
            ACT(fS[:, i, :], PS[pf][:, :], AF.Copy, [tps[pf]], [tfS[i]])
            ACT(sqbuf[:, 0:512], PS[pq][:, :], AF.Square, [tps[pq]], [tsq])
            ACT(sqbuf[:, 512:640], PS[pkv][:, 0:128], AF.Square, [tps[pkv]], [tsq])
            S.op("dve", lambda e: e.tensor_reduce(out=msb[:, 0:10], in_=sqbuf[:, :].rearrange("p (h d) -> p h d", d=64),
                                                  op=ALU.add, axis=AX.X), [tsq], [tms])
            TS(msb[:, 0:10], msb[:, 0:10], 1.0 / 64.0, EPS, ALU.mult, ALU.add, [tms], [tms])
            ACT(msb[:, 16:26], msb[:, 0:10], AF.Ln, [tms], [tms])
            ACT(msb[:, 32:42], msb[:, 16:26], AF.Exp, [tms], [tms], scale=-0.5)
            q_ = qn[b2]
            TT(q_[:, 0:8, :], PS[pq][:, :].rearrange("p (h d) -> p h d", d=64),
               msb[:, 32:40].unsqueeze(2).broadcast_to([128, 8, 64]), ALU.mult, [tps[pq], tms], [tqn[b2]])
            TT(q_[:, 8:10, :], PS[pkv][:, 0:128].rearrange("p (h d) -> p h d", d=64),
               msb[:, 40:42].unsqueeze(2).broadcast_to([128, 2, 64]), ALU.mult, [tps[pkv], tms], [tqn[b2]])
            TT(q_[:, :, :], q_[:, :, :], gqk10[:, :, :], ALU.mult, [tqn[b2], tg10], [tqn[b2]])
            if h == 0:
                S.dma("sp", lambda e, q_=q_, i=i: e.dma_start(
                    out=nk_out[i * 128:(i + 1) * 128, :], in_=q_[:, 8:10, :].rearrange("p h d -> p (h d)")),
                    [tqn[b2]], (), is_output=True)
                ACT(vst[b2][:, :], PS[pkv][:, 128:256], AF.Copy, [tps[pkv]], [tvst[b2]])
                S.dma("sp", lambda e, b2=b2, i=i: e.dma_start(out=nv_out[i * 128:(i + 1) * 128, :], in_=vst[b2][:, :]),
                      [tvst[b2]], (), is_output=True)
            else:
                cosb = vslice("cosF", i * 64, 64).unsqueeze(1).broadcast_to([128, 10, 64])
                TT(rt1[:, :, :], q_[:, :, :], cosb, ALU.mult, [tqn[b2], tvec], [trt1])
                for g in range(2):
                    lo = slice(g * 32, g * 32 + 16)
                    hi = slice(g * 32 + 16, g * 32 + 32)
                    s_lo = vslice("sinF", i * 64 + g * 32, 16).unsqueeze(1).broadcast_to([128, 10, 16])
                    s_hi = vslice("sinF", i * 64 + g * 32 + 16, 16).unsqueeze(1).broadcast_to([128, 10, 16])
                    TT(rt2[:, :, lo], q_[:, :, hi], s_lo, ALU.mult, [tqn[b2], tvec], [trt2])
                    TT(rt2[:, :, hi], q_[:, :, lo], s_hi, ALU.mult, [tqn[b2], tvec], [trt2])
                TT(q_[:, :, :], rt1[:, :, :], rt2[:, :, :], ALU.add, [trt1, trt2], [tqn[b2]])
            ACT(q16k[b2][:, 0:4, :], q_[:, 0:8, :].rearrange("p (j a) d -> p j (a d)", a=2), AF.Copy,
                [tqn[b2]], [tq16[b2]])
            for kvh in range(2):
                CP(q16k[b2][:, 4 + kvh, :].rearrange("p (a d) -> p a d", d=64),
                   q_[:, 8 + kvh, :].unsqueeze(1).broadcast_to([128, 2, 64]), [tqn[b2]], [tq16[b2]])
            vsrc = PS[pkv][:, 128:256].rearrange("p (k d) -> p k d", d=64)
            ACT(vaug[:, i, :, 0:64], vsrc, AF.Copy, [tps[pkv]], [tva[i]])
            ACT(vaug[:, i, :, 128:192], vsrc, AF.Copy, [tps[pkv]], [tva[i]])
            def do_tr(i=i, b2=b2):
                ti = trot.next()
                ptr = PS[ti][:, :].bitcast(BF16)[:, 0:768].rearrange("p (m n) -> p m n", n=128)
                for m in range(6):
                    TR(ptr[:, m, :], q16k[b2][:, m, :], [tq16[b2]], [tps[ti]])
                CP(qT[:, :, i * 128:(i + 1) * 128], ptr[:, 0:4, :], [tps[ti]], [tqT[i]])
                CP(kT[0:64, :, 0, i * 128:(i + 1) * 128], ptr[0:64, 4:6, :], [tps[ti]], [tkT[i]])
                CP(kT[64:128, :, 1, i * 128:(i + 1) * 128], ptr[64:128, 4:6, :], [tps[ti]], [tkT[i]])
            if FLAG_DEFER_TR:
                pend_tr.append(do_tr)
            else:
                do_tr()
            if h == 0 and i % 2 == 1:
                bg_step()
        while pend_tr:
            pend_tr.pop(0)()
        for w_ in (wf, wq, wkv):
            unpin(w_[0])

        if h == 0:
            nseq, Sq, nst = 4, 256, 2
            loads = [(cs, 0) for cs in range(2)]
        else:
            nseq, Sq, nst = 1, 1024, 8
            loads = [(cs, sbk) for cs in range(2) for sbk in range(2)]
        N1 = min(Sq, 512)
        for (cs, sbk) in loads:
            if h == 0:
                slot, tl = wload([(dftp_in[cs][:, :], 2, 0, 256)])
            else:
                slot, tl = wload([(dfts_in[cs][:, sbk * 512:(sbk + 1) * 512], 8, 0, 512)])
            for sq_ in range(nseq):
                for c in range(4):
                    pi = mainrot.next()
                    for st_ in range(nst):
                        ti_ = sq_ * nst + st_
                        MM(PS[pi][:, 0:N1], fS[:, ti_, c * 128:(c + 1) * 128], slot[:, st_, 0:N1], st_ == 0,
                           st_ == nst - 1, tl + [tfS[ti_]], [tps[pi]])
                    t0 = sq_ * Sq + sbk * 512
                    blk = t0 // 512
                    ACT(Gb[:, cs, c, t0:t0 + N1], PS[pi][:, 0:N1], AF.Copy, [tps[pi]], [tG[cs][c][blk]])
        for c in range(4):
            for blk in range(2):
                pi = mainrot.next()
                MM(PS[pi][:, :], BDc, Gb[:, 0, c, blk * 512:(blk + 1) * 512], True, False, [tcbf, tG[0][c][blk]], [tps[pi]])
                MM(PS[pi][:, :], BDsn, Gb[:, 1, c, blk * 512:(blk + 1) * 512], False, True, [tcbf, tG[1][c][blk]], [tps[pi]])
                CP(actA[:, c, blk * 512:(blk + 1) * 512], PS[pi][:, :], [tps[pi]], [tA[c][blk]])

        if h == 0:
            srot = Rot([(0,), (1,), (2,)])
        else:
            srot = Rot([(0, 1), (2, 3)])
        orot = Rot([4, 5])
        brot = Rot([6, 7])
        lrot = Rot([0, 1])
        p2rot = Rot([0, 1])
        pending_norm = []
        if h == 0:
            jobs = [(sq_ * 256, 256, [2 * sq_, 2 * sq_ + 1]) for sq_ in range(4)]
        else:
            jobs = [(qb * 512, 512, list(range(12))) for qb in range(2)]
        steps = []
        for (q0, Nq, ktiles) in jobs:
            for hd in range(8):
                hj = {"q0": q0, "Nq": Nq, "ktiles": ktiles, "hd": hd, "kvh": hd // 4, "j": hd // 2,
                      "base": (hd % 2) * 64, "npair": len(ktiles) // 2, "po": None, "blk": q0 // 512,
                      "qtl": [tqT[t] for t in range(q0 // 128, (q0 + Nq) // 128)]}
                for pi_ in range(hj["npair"]):
                    steps.append({"hj": hj, "pi": pi_, "bks": None})

        def qk2(st_):
            hj = st_["hj"]
            bks = srot.next()
            st_["bks"] = bks
            for u_ in range(2):
                kt = hj["ktiles"][2 * st_["pi"] + u_]
                if h == 0:
                    dst = PS[bks[0]][:, u_ * 256:(u_ + 1) * 256]
                    tb = tps[bks[0]]
                else:
                    dst = PS[bks[u_]][:, :]
                    tb = tps[bks[u_]]
                MM(dst, kT[:, hj["kvh"], hj["hd"] % 2, kt * 128:(kt + 1) * 128],
                   qT[:, hj["j"], hj["q0"]:hj["q0"] + hj["Nq"]], True, True, [tkT[kt], tkz] + hj["qtl"], [tb])

        def pv2(st_):
            hj = st_["hj"]
            Nq = hj["Nq"]
            if st_["pi"] == 0:
                hj["po"] = orot.next()
            po = hj["po"]
            bks = st_["bks"]
            pb_ = p2rot.next()
            if h == 0:
                src = PS[bks[0]][:, :]
            else:
                src = PSall[:, bks[0] * 512:(bks[0] + 2) * 512]
            ACT(pT2[pb_][:, 0:2 * Nq], src, AF.Exp, [tps[b_] for b_ in bks] + [tnegb], [tpT2[pb_]],
                bias=negb[:, 0:1])
            for u_ in range(2):
                ci = 2 * st_["pi"] + u_
                kt = hj["ktiles"][ci]
                rhs_ = pT2[pb_][:, u_ * Nq:(u_ + 1) * Nq]
                if hj["base"] == 0:
                    MM(PS[po][0:65, 0:Nq], vaug[:, kt, hj["kvh"], 0:65], rhs_, ci == 0, ci == 2 * hj["npair"] - 1,
                       [tva[kt], tpT2[pb_]], [tps[po]])
                else:
                    MM(PS[po][:, 0:Nq], vaug[:, kt, hj["kvh"], 64:192], rhs_, ci == 0, ci == 2 * hj["npair"] - 1,
                       [tva[kt], tpT2[pb_]], [tps[po]])

        def make_norm(hj):
            po, base, j, q0, Nq, blk = hj["po"], hj["base"], hj["j"], hj["q0"], hj["Nq"], hj["blk"]

            def norm():
                rp = 64 if base == 0 else 0
                li = lrot.next()
                ACT(lnrow[li][rp:rp + 1, 0:Nq], PS[po][rp:rp + 1, 0:Nq], AF.Ln, [tps[po]], [tlnrow[li]])
                bi = brot.next()
                if base == 0:
                    MM(PS[bi][0:64, 0:Nq], onesf[64:65, 0:64], lnrow[li][64:65, 0:Nq], True, True,
                       [tonesf, tlnrow[li]], [tps[bi]])
                else:
                    MM(PS[bi][:, 0:Nq], onesf[0:1, :], lnrow[li][0:1, 0:Nq], True, True,
                       [tonesf, tlnrow[li]], [tps[bi]])
                ACT(recb[li][base:base + 64, 0:Nq], PS[bi][base:base + 64, 0:Nq], AF.Exp, [tps[bi]], [trecb[li]],
                    scale=-1.0)
                TT(actA[base:base + 64, 4 + j, q0:q0 + Nq], PS[po][base:base + 64, 0:Nq],
                   recb[li][base:base + 64, 0:Nq], ALU.mult, [tps[po], trecb[li]], [tA[4 + j][blk]])
            return norm

        qk2(steps[0])
        for si_, st_ in enumerate(steps):
            if si_ + 1 < len(steps):
                qk2(steps[si_ + 1])
            pv2(st_)
            while pending_norm:
                pending_norm.pop(0)()
            hj = st_["hj"]
            if st_["pi"] == hj["npair"] - 1:
                pending_norm.append(make_norm(hj))
                if h == 0 and hj["hd"] in (3, 7):
                    bg_step(bank=3)
        while pending_norm:
            pending_norm.pop(0)()

        if h == 0:
            bg_until(12)
            derive(0)
            bg_issue()
        outproj_ln(0, cnd, w_out_a[0], 8, lambda kc, blk: (actA[:, kc, blk * 512:(blk + 1) * 512], tA[kc][blk]),
                   G1, LG1, LB1, LGA, LBA)

    def mixer_conv(h):
        cnd = h
        l = 1
        for i in range(6):
            MS(convbuf[i][:, :], 0.0, tconv2[i])

        def load_c(c):
            return wload([(w_in_c[0][:, c * 128:(c + 1) * 128], 8, 0, 128),
                          (w_in_c[0][:, D + c * 128:D + (c + 1) * 128], 8, 128, 128),
                          (w_in_c[0][:, 2 * D + c * 128:2 * D + (c + 1) * 128], 8, 256, 128)], pin=True)
        order = [(0, 0), (1, 0), (0, 1), (1, 1)] + [(c_, b_) for c_ in range(2, 8) for b_ in range(2)]
        loaded = {}

        def ensure(c_):
            if c_ < 8 and c_ not in loaded:
                loaded[c_] = load_c(c_)
        ensure(0)
        ensure(1)
        for (c, blk) in order:
            ensure(c)
            ensure(c + 1)
            ensure(c + 2)
            slot, tl = loaded[c]
            s3 = 3 * (c % 2)
            ub, yb, bgb = convbuf[s3], convbuf[s3 + 1], convbuf[s3 + 2]
            tu, ty, tbg = tconv2[s3], tconv2[s3 + 1], tconv2[s3 + 2]
            pis = []
            for part in range(3):
                pi = rot8.next()
                pis.append(pi)
                for k in range(8):
                    MM(PS[pi][:, :], slot[:, k, part * 128:(part + 1) * 128], actA[:, k, blk * 512:(blk + 1) * 512],
                       k == 0, k == 7, tl + [tA[k][blk]], [tps[pi]])
            ACT(seg_view(bgb, h, blk), ps_view(PS[pis[0]][:, :], h), AF.Copy, [tps[pis[0]]], [tbg[blk]])
            ACT(seg_view(yb, h, blk), ps_view(PS[pis[2]][:, :], h), AF.Copy, [tps[pis[2]]], [ty[blk]])
            TT(seg_view(ub, h, blk), ps_view(PS[pis[1]][:, :], h), seg_view(yb, h, blk), ALU.mult,
               [tps[pis[1]], ty[blk]], [tu[blk]])
            if blk == 1:
                unpin(slot)
                wv = [vec[:, VOFF["convc"][0] + k * 8 + c: VOFF["convc"][0] + k * 8 + c + 1] for k in range(3)]
                conv3(ub, tu, yb, ty, wv[0], wv[1], wv[2], tvec, h)
                TT(tok_flat(actT[:, c, :], h), tok_view(yb, h), tok_view(bgb, h), ALU.mult, ty + tbg,
                   [tact[c][0], tact[c][1]])
        outproj_ln(1, cnd, w_out_c[0], 8, lambda kc, blk: (actT[:, kc, blk * 512:(blk + 1) * 512], tact[kc][blk]),
                   G1, LG1, LB1, LGA, LBA)

    def ffn(l, h, last):
        cnd = h
        o0 = VOFF["convf"][0] + l * 132
        yrot = Rot(range(4))
        pending_tail = []

        def run_tail():
            while pending_tail:
                pending_tail.pop(0)()

        def taps_blk(y, ty2, banks, w0, w2, blk):
            P = PS[banks[blk]]
            rd = [tps[banks[blk]], ty2[blk], tvec]
            wr = [ty2[blk]]
            if h == 0:
                P3 = P[:, :].rearrange("p (s w) -> p s w", w=256)
                Y3 = y[:, blk * 512:(blk + 1) * 512].rearrange("p (s w) -> p s w", w=256)
                STT(Y3[:, :, 1:256], P3[:, :, 0:255], w0, Y3[:, :, 1:256], ALU.mult, ALU.add, rd, wr)
                STT(Y3[:, :, 0:255], P3[:, :, 1:256], w2, Y3[:, :, 0:255], ALU.mult, ALU.add, rd, wr)
            else:
                b0 = blk * 512
                STT(y[:, b0 + 1:b0 + 512], P[:, 0:511], w0, y[:, b0 + 1:b0 + 512], ALU.mult, ALU.add, rd, wr)
                STT(y[:, b0:b0 + 511], P[:, 1:512], w2, y[:, b0:b0 + 511], ALU.mult, ALU.add, rd, wr)

        erot = Rot(range(8))
        edge_idx = {}
        RRf = RR[:, cb0:cb0 + 8 * UW * 2].bitcast(F32)
        seam_rot = Rot(range(2))

        def quad(ap, n, col):
            return ap.rearrange("p (j a n) -> p a j n", j=2, a=2, n=n)[:, :, :, col]

        def seam_save(s_, b0):
            ei = erot.next()
            edge_idx[s_] = ei
            dst = tmp_small[:, 160 + 4 * ei:164 + 4 * ei].rearrange("p (a j) -> p a j", j=2)
            CP(dst, quad(PSall[:, b0 * 512:(b0 + 4) * 512], 512, 511), [tps[b0 + i_] for i_ in range(4)], [tedge[ei]])

        def seam_apply(s_, b1, ys0, c0, info_):
            ei = edge_idx[s_]
            E = tmp_small[:, 160 + 4 * ei:164 + 4 * ei].rearrange("p (a j) -> p a j", j=2)
            W0 = vec[:, o0 + c0:o0 + c0 + 44].rearrange("p (a x) -> p a x", x=22)[:, :, 0:2]
            W2 = vec[:, o0 + 88 + c0:o0 + 88 + c0 + 44].rearrange("p (a x) -> p a x", x=22)[:, :, 0:2]
            Yq = RRf[:, 2 * ys0 * UW:(2 * ys0 + 4) * UW]
            Y512, Y511 = quad(Yq, UW, 512), quad(Yq, UW, 511)
            P10 = quad(PSall[:, b1 * 512:(b1 + 4) * 512], 512, 0)
            k_ = seam_rot.next()
            tA_ = tmp_small[:, 224 + 8 * k_:228 + 8 * k_].rearrange("p (a j) -> p a j", j=2)
            tB_ = tmp_small[:, 228 + 8 * k_:232 + 8 * k_].rearrange("p (a j) -> p a j", j=2)
            ty1 = [info_[(jj_, pt_)][1][1] for jj_ in range(2) for pt_ in range(2)]
            ty0 = [info_[(jj_, pt_)][1][0] for jj_ in range(2) for pt_ in range(2)]
            TT(tA_, E, W0, ALU.mult, [tedge[ei], tvec, tseam[k_]], [tseam[k_]])
            TT(Y512, Y512, tA_, ALU.add, [tseam[k_]] + ty1, ty1)
            TT(tB_, P10, W2, ALU.mult, [tps[b1 + i_] for i_ in range(4)] + [tvec, tseam[k_]], [tseam[k_]])
            TT(Y511, Y511, tB_, ALU.add, [tseam[k_]] + ty0, ty0)

        def load_slot(s):
            return wload([(w_up[l][:, s * 256:(s + 1) * 256], 8, 0, 256),
                          (w_up[l][:, DFF + s * 256:DFF + (s + 1) * 256], 8, 256, 256)], pin=True)
        order = [(0, 0), (1, 0), (0, 1), (1, 1)] + [(s_, b_) for s_ in range(2, 11) for b_ in range(2)]
        loaded = {}
        infos = {}
        bankmap = {}

        def ensure(s_):
            if s_ < 11 and s_ not in loaded:
                pf = prefetched_up.get((l, cnd), {})
                if s_ in pf:
                    loaded[s_] = pf.pop(s_)
                else:
                    loaded[s_] = load_slot(s_)
        ensure(0)
        ensure(1)
        for (s, blk) in order:
            ensure(s)
            ensure(s + 1)
            if not (l == 0 and h == 0) and s < 8:
                ensure(s + 2)
            slot, tl = loaded[s]
            if s not in infos:
                info = {}
                for jj in range(2):
                    c = 2 * s + jj
                    ys = yrot.next()
                    if jj == 0:
                        info["ys0"] = ys
                    wa = [vec[:, o0 + k * 44 + c: o0 + k * 44 + c + 1] for k in range(3)]
                    wg = [vec[:, o0 + k * 44 + 22 + c: o0 + k * 44 + 22 + c + 1] for k in range(3)]
                    info[(jj, 0)] = (convbuf[2 * ys], tconv2[2 * ys], wa)
                    info[(jj, 1)] = (convbuf[2 * ys + 1], tconv2[2 * ys + 1], wg)
                infos[s] = info
            info = infos[s]
            for jj in range(2):
                for part in range(2):
                    pi = rot8.next()
                    bankmap[(s, jj, part, blk)] = pi
                    for k in range(8):
                        MM(PS[pi][:, :], slot[:, k, part * 256 + jj * 128:part * 256 + (jj + 1) * 128],
                           actA[:, k, blk * 512:(blk + 1) * 512], k == 0, k == 7, tl + [tA[k][blk]], [tps[pi]])
            for jj in range(2):
                for part in range(2):
                    y, ty2, w = info[(jj, part)]
                    bk = [bankmap.get((s, jj, part, 0)), bankmap.get((s, jj, part, 1))]
                    ACT(y[:, blk * 512:(blk + 1) * 512], PS[bk[blk]][:, :], AF.Identity,
                        [tps[bk[blk]], tvec], [ty2[blk]], scale=w[1])
                    taps_blk(y, ty2, bk, w[0], w[2], blk)
            if h == 1:
                bq = bankmap[(s, 0, 0, blk)]
                assert all(bankmap[(s, jj_, pt_, blk)] == bq + 2 * jj_ + pt_ for jj_ in range(2) for pt_ in range(2))
                assert info["ys0"] % 2 == 0
                if blk == 0:
                    seam_save(s, bq)
                else:
                    seam_apply(s, bq, info["ys0"], 2 * s, info)
            run_tail()
            if blk == 1:
                unpin(slot)
                for jj in range(2):
                    def tail(c=2 * s + jj, ya=info[(jj, 0)][0], yg=info[(jj, 1)][0], tya=info[(jj, 0)][1],
                             tyg=info[(jj, 1)][1]):
                        ACT(yg[:, 0:T], yg[:, 0:T], AF.Silu, tyg, tyg)
                        TT(actT[:, c, :], yg[:, 0:T], ya[:, 0:T], ALU.mult, tyg + tya, [tact[c][0], tact[c][1]],
                           eng="pool")
                    pending_tail.append(tail)
                if l == 0 and h == 0:
                    bg_step()
                if s >= 8 and len(pinned) < NSLOT - 1:
                    kg = len(wdown_pref.setdefault((l, cnd), []))
                    if kg < 3:
                        k0 = kg * 8
                        nk = min(8, NHC - k0)
                        wdown_pref[(l, cnd)].append(
                            wload([(w_down[l][k0 * 128:(k0 + nk) * 128, 0:512], nk, 0, 512)], pin=True))
        run_tail()
        if l == 0 and h == 0:
            bg_until(24)
            derive_in(1)
            derive(1)
            derive_cross()
        if last:
            outproj_ln(l, cnd, w_down[l], NHC, lambda kc, blk: (actT[:, kc, blk * 512:(blk + 1) * 512], tact[kc][blk]),
                       G2, LG2, LB2, None, None)
        else:
            outproj_ln(l, cnd, w_down[l], NHC, lambda kc, blk: (actT[:, kc, blk * 512:(blk + 1) * 512], tact[kc][blk]),
                       G2, LG2, LB2, LGB, LBB)

    bg_until(4)
    derive_in(0)
    for h in range(2):
        load_x(h)
        modulate_in(0, h)
        mixer_even(h)
        handoff(mix_tiles, ffn_tiles)
        ffn(0, h, last=False)
        mixer_conv(h)
        ffn(1, h, last=True)
        store_y(h)
        handoff(ffn_tiles, mix_tiles)
    S.finish()
    return nc


_CACHE = {}


def _consts():
    if "c" in _CACHE:
        return _CACHE["c"]
    bf = ml_dtypes.bfloat16
    cb = np.zeros((128, 512), np.float32)
    cb[:, 0:128] = np.eye(128, dtype=np.float32)
    cb[:, 128:256] = 1.0 / 1024.0
    n = np.arange(64)
    ang = 2.0 * np.pi * np.outer(n, n) / 64.0
    c64 = np.cos(ang) / 8.0
    s64 = np.sin(ang) / 8.0
    bdc = np.zeros((128, 128))
    bds = np.zeros((128, 128))
    for g in range(2):
        bdc[g * 64:(g + 1) * 64, g * 64:(g + 1) * 64] = c64
        bds[g * 64:(g + 1) * 64, g * 64:(g + 1) * 64] = -s64
    cb[:, 256:384] = bdc
    cb[:, 384:512] = bds
    cbf = cb.astype(bf)

    def dft(Sn):
        k = np.arange(Sn, dtype=np.int64)
        m = (np.outer(k, k) % Sn).astype(np.float64)
        a = 2.0 * np.pi * m / Sn
        return np.stack([np.cos(a), np.sin(a)]).astype(np.float32) / np.float32(math.sqrt(Sn))
    dftp = np.ascontiguousarray(dft(256))
    dfts = np.ascontiguousarray(dft(1024))
    t = np.arange(1024)
    row = (t // 64).astype(np.float32)
    col = (t % 64).astype(np.float32)
    half = 32
    inv = (1.0 / (10000.0 ** (np.arange(0, half, 2, dtype=np.float32) / half))).astype(np.float32)
    ar = row[:, None] * inv[None, :]
    ac = col[:, None] * inv[None, :]
    cosF = np.concatenate([np.cos(ar), np.cos(ar), np.cos(ac), np.cos(ac)], axis=1).astype(np.float32)
    sinF = np.concatenate([-np.sin(ar), np.sin(ar), -np.sin(ac), np.sin(ac)], axis=1).astype(np.float32)
    cosF = cosF.reshape(8, 128, 64).transpose(1, 0, 2).reshape(128, 512)
    sinF = sinF.reshape(8, 128, 64).transpose(1, 0, 2).reshape(128, 512)
    _CACHE["c"] = (cbf, dftp, dfts, cosF, sinF)
    return _CACHE["c"]


def _pp(v):
    v = np.asarray(v, np.float32)
    lead = int(np.prod(v.shape[:-1])) if v.ndim > 1 else 1
    n = v.shape[-1] // 128
    return np.ascontiguousarray(v.reshape(lead, n, 128).transpose(2, 0, 1).reshape(128, lead * n))


def kernel(x_prompt, x_sample, cache_k, cache_v, c, c_ctx, w_ada, b_ada, ln_g, ln_b,
           w_in_a, q_norm_g, k_norm_g, w_out_a, w_in_c, conv_c, w_out_c, w_up, conv_f, w_down):
    f32 = np.float32
    cbf, dftp, dfts, cosF, sinF = _consts()
    if "nc" not in _CACHE:
        _CACHE["nc"] = build_program()
    nc = _CACHE["nc"]
    x_prompt = np.asarray(x_prompt, f32)
    x_sample = np.asarray(x_sample, f32)
    cache_k = np.asarray(cache_k, f32)
    cache_v = np.asarray(cache_v, f32)
    c = np.asarray(c, f32)
    c_ctx = np.asarray(c_ctx, f32)
    shared = {
        "cbf": cbf, "dftp": dftp, "dfts": dfts,
        "w_ada": np.ascontiguousarray(np.asarray(w_ada, f32)),
        "w_in_a": np.ascontiguousarray(np.asarray(w_in_a, f32)),
        "w_out_a": np.ascontiguousarray(np.asarray(w_out_a, f32)),
        "w_in_c": np.ascontiguousarray(np.asarray(w_in_c, f32)),
        "w_out_c": np.ascontiguousarray(np.asarray(w_out_c, f32)),
        "w_up": np.ascontiguousarray(np.asarray(w_up, f32)),
        "w_down": np.ascontiguousarray(np.asarray(w_down, f32)),
    }
    bada = _pp(np.asarray(b_ada, f32))
    lng = _pp(np.asarray(ln_g, f32))
    lnb = _pp(np.asarray(ln_b, f32))
    convf = _pp(np.asarray(conv_f, f32))
    convc = _pp(np.asarray(conv_c, f32))
    gqk = np.concatenate([np.asarray(q_norm_g, f32).reshape(1, 64), np.asarray(k_norm_g, f32).reshape(1, 64)], axis=1)
    gqk = np.ascontiguousarray(np.broadcast_to(gqk, (128, 128)))
    in_maps = []
    for core in range(NCORES):
        xp = x_prompt[core * 4:(core + 1) * 4].reshape(T, D)
        xs = x_sample[core].reshape(T, D)
        xa = np.concatenate([xp, xs], axis=0)
        xTh = np.ascontiguousarray(xa.reshape(2 * T, 8, 128).transpose(2, 1, 0))
        cond2 = np.stack([c_ctx, c[core]], axis=0)
        condT = np.ascontiguousarray(cond2.reshape(2, 8, 128).transpose(2, 1, 0).reshape(128, 16))
        vecs = np.concatenate([bada, lng, lnb, convf, convc, condT, gqk, cosF, sinF], axis=1).astype(f32)
        assert vecs.shape == (128, NVEC), vecs.shape
        ck = cache_k[core, 0]
        ckT = np.ascontiguousarray(np.tile(ck.transpose(2, 1, 0), (2, 1, 1)))
        cv = np.ascontiguousarray(cache_v[core, 0].reshape(4, 128, 128).transpose(1, 0, 2))
        m = dict(shared)
        m.update({"xT": xTh, "vecs": np.ascontiguousarray(vecs), "ckT": ckT, "cv": cv})
        in_maps.append(m)
    res = run_bass_kernel_spmd(nc, in_maps, core_ids=list(range(NCORES)))
    y_prompt = np.empty((32, 256, D), f32)
    y_sample = np.empty((8, 1024, D), f32)
    nk = np.empty((32, 1, 256, 2, 64), f32)
    nv = np.empty((32, 1, 256, 2, 64), f32)
    for core in range(NCORES):
        r = res.results[core]
        yT = np.asarray(r["yT"], f32)
        ya = yT.transpose(2, 1, 0).reshape(2 * T, D)
        y_prompt[core * 4:(core + 1) * 4] = ya[:T].reshape(4, 256, D)
        y_sample[core] = ya[T:]
        nk[core * 4:(core + 1) * 4, 0] = np.asarray(r["nk"], f32).reshape(4, 256, 2, 64)
        nv[core * 4:(core + 1) * 4, 0] = np.asarray(r["nv"], f32).reshape(4, 256, 2, 64)
    return (y_prompt, y_sample, nk, nv)
```

```python
import math
import numpy as np
import ml_dtypes
import concourse.bass as bass
import concourse.mybir as mybir
from concourse.bass_utils import run_bass_kernel_spmd

F32 = mybir.dt.float32
BF16 = mybir.dt.bfloat16
ALU = mybir.AluOpType
AF = mybir.ActivationFunctionType
AX = mybir.AxisListType

FLAG_POOLTAP = False
FLAG_BG = True
FLAG_DEFER_TR = True
FLAG_BLKOUTER = True

NCORES = 8
D = 1024
T = 1024
DFF = 2816
NHC = 22
ALPHA = 4.0 ** 0.25
EPS = 1e-6
EPS_LN = EPS / (ALPHA * ALPHA)
COMPUTE = ("pe", "act", "dve", "pool")


class Tile:
    __slots__ = ("name", "last_w", "readers", "excl")

    def __init__(self, name="", excl=False):
        self.name = name
        self.last_w = None
        self.readers = []
        self.excl = excl


class Op:
    __slots__ = ("eng", "fn", "deps", "is_dma", "sem", "semval", "milestone", "mcount", "idx")

    def __init__(self, eng, fn, is_dma):
        self.idx = 0
        self.eng = eng
        self.fn = fn
        self.deps = []
        self.is_dma = is_dma
        self.sem = None
        self.semval = 0
        self.milestone = False
        self.mcount = 0


class Sched:
    def __init__(self, nc, n_dma_sems=24):
        self.nc = nc
        self.ops = {e: [] for e in ("pe", "act", "dve", "pool", "sp")}
        self.esem = {e: nc.alloc_semaphore("sem_" + e) for e in COMPUTE}
        self.dsems = [nc.alloc_semaphore("dsem%d" % i) for i in range(n_dma_sems)]
        self.dsem_last = [None] * n_dma_sems
        self.dsem_uses = [0] * n_dma_sems
        half = n_dma_sems // 2
        self.dq = {"pool": [list(range(0, half)), 0], "sp": [list(range(half, n_dma_sems)), 0]}
        self.out_dmas = []
        self.nops = 0

    def _add(self, eng, fn, reads, writes, is_dma):
        op = Op(eng, fn, is_dma)
        self.nops += 1
        op.idx = self.nops
        deps = {}
        for t in reads:
            if t.last_w is not None:
                deps[id(t.last_w)] = t.last_w
            if t.excl:
                for r in t.readers:
                    if r.is_dma or r.eng != eng:
                        deps[id(r)] = r
        for t in writes:
            if t.last_w is not None:
                deps[id(t.last_w)] = t.last_w
            for r in t.readers:
                deps[id(r)] = r
        best = {}
        for d in deps.values():
            if d is op:
                continue
            if (not is_dma) and (not d.is_dma) and d.eng == "pe" and eng == "pe":
                continue
            if d.is_dma:
                op.deps.append(d)
            else:
                b = best.get(d.eng)
                if b is None or d.idx > b.idx:
                    best[d.eng] = d
        for d in best.values():
            d.milestone = True
            op.deps.append(d)
        if is_dma:
            q = self.dq[eng]
            i = q[0][q[1] % len(q[0])]
            q[1] += 1
            prev = self.dsem_last[i]
            if prev is not None:
                op.deps.append(prev)
            self.dsem_uses[i] += 1
            op.sem = self.dsems[i]
            op.semval = 16 * self.dsem_uses[i]
            self.dsem_last[i] = op
        for t in reads:
            if not is_dma:
                t.readers = [r for r in t.readers if r.is_dma or r.eng != eng]
            t.readers.append(op)
        for t in writes:
            t.last_w = op
            t.readers = []
        self.ops[eng].append(op)
        return op

    def op(self, eng, fn, reads=(), writes=()):
        return self._add(eng, fn, reads, writes, False)

    def dma(self, eng, fn, reads=(), writes=(), is_output=False):
        o = self._add(eng, fn, reads, writes, True)
        if is_output:
            self.out_dmas.append(o)
        return o

    def finish(self):
        nc = self.nc
        for e in COMPUTE:
            if self.ops[e]:
                self.ops[e][-1].milestone = True
        final_deps = list(self.out_dmas) + [self.ops[e][-1] for e in COMPUTE
                                            if self.ops[e] and not self.ops[e][-1].is_dma]
        for d in self.dsem_last:
            if d is not None:
                final_deps.append(d)
        fin = Op("sp", None, False)
        fin.deps = final_deps
        self.ops["sp"].append(fin)
        for e in COMPUTE:
            c = 0
            for o in self.ops[e]:
                if o.milestone and not o.is_dma:
                    c += 1
                    o.mcount = c
        esem = self.esem

        def emit_engine(ename, eng):
            seen = {}
            for o in self.ops[ename]:
                waits = {}
                for d in o.deps:
                    if d.is_dma:
                        key = ("d", id(d.sem))
                        sem, val = d.sem, d.semval
                    else:
                        key = ("e", d.eng)
                        sem, val = esem[d.eng], d.mcount
                    if seen.get(key, 0) >= val:
                        continue
                    if key not in waits or waits[key][1] < val:
                        waits[key] = (sem, val)
                for key, (sem, val) in waits.items():
                    eng.wait_ge(sem, val)
                    seen[key] = val
                if o.fn is None:
                    continue
                ins = o.fn(eng)
                if o.is_dma:
                    ins.then_inc(o.sem, 16)
                elif o.milestone:
                    ins.then_inc(esem[ename], 1)

        with nc.Block() as block:
            @block.tensor
            def _(e):
                emit_engine("pe", e)

            @block.scalar
            def _(e):
                emit_engine("act", e)

            @block.vector
            def _(e):
                emit_engine("dve", e)

            @block.gpsimd
            def _(e):
                emit_engine("pool", e)

            @block.sync
            def _(e):
                emit_engine("sp", e)


class Rot:
    def __init__(self, items):
        self.items = list(items)
        self.i = 0

    def next(self):
        x = self.items[self.i % len(self.items)]
        self.i += 1
        return x


def _vec_layout():
    off = {}
    o = 0
    for name, n in (("bada", 96), ("lng", 32), ("lnb", 32), ("convf", 264), ("convc", 24),
                    ("cond", 16), ("gqk", 128), ("cosF", 512), ("sinF", 512)):
        off[name] = (o, n)
        o += n
    return off, o


VOFF, NVEC = _vec_layout()


def build_program():
    nc = bass.Bass("TRN2", target_bir_lowering=False)
    S = Sched(nc)

    def din(name, shape, dt=F32):
        return nc.dram_tensor(name, list(shape), dt, kind="ExternalInput").ap()

    def dout(name, shape, dt=F32):
        return nc.dram_tensor(name, list(shape), dt, kind="ExternalOutput").ap()

    x_in = din("xT", [128, 8, 2 * T])
    vecs_in = din("vecs", [128, NVEC])
    cbf_in = din("cbf", [128, 512], BF16)
    ck_in = din("ckT", [128, 2, 512])
    cv_in = din("cv", [128, 4, 128])
    dftp_in = din("dftp", [2, 256, 256])
    dfts_in = din("dfts", [2, 1024, 1024])
    w_ada = din("w_ada", [2, D, 6 * D])
    w_in_a = din("w_in_a", [1, D, 1280])
    w_out_a = din("w_out_a", [1, D, D])
    w_in_c = din("w_in_c", [1, D, 3 * D])
    w_out_c = din("w_out_c", [1, D, D])
    w_up = din("w_up", [2, D, 2 * DFF])
    w_down = din("w_down", [2, DFF, D])
    y_out = dout("yT", [128, 8, 2 * T])
    nk_out = dout("nk", [T, 128])
    nv_out = dout("nv", [T, 128])

    sb = nc.alloc_sbuf_tensor
    xT = sb("xTs", [128, 8, T], F32)
    tx = [[Tile("x%d_%d" % (c, b)) for b in range(2)] for c in range(8)]
    actA = sb("actA", [128, 8, T], BF16)
    tA = [[Tile("A%d_%d" % (c, b)) for b in range(2)] for c in range(8)]
    NSLOT = 6
    slots = [sb("slot%d" % i, [128, 8, 512], BF16) for i in range(NSLOT)]
    tslot = [[Tile("s%d_lo" % i), Tile("s%d_hi" % i)] for i in range(NSLOT)]
    slot_rot = Rot(range(NSLOT))
    vec = sb("vec", [128, NVEC], F32)
    tvec = Tile("vec")
    cbf = sb("cbfs", [128, 512], BF16)
    tcbf = Tile("cbf")
    identb = cbf[:, 0:128]
    onesD = cbf[:, 128:256]
    BDc = cbf[:, 256:384]
    BDsn = cbf[:, 384:512]
    onesf = sb("onesf", [128, 128], F32)
    tonesf = Tile("onesf")
    mod_sb = sb("mod_sb", [128, 2, 96], F32)
    tmod = [Tile("mod0"), Tile("mod1")]
    NDV = 16
    dv = sb("dv", [128, 2, 2, NDV, 8], F32)
    tdv = [[Tile("dv%d%d" % (l, c)) for c in range(2)] for l in range(2)]
    scT = sb("scT", [128, 16], BF16)
    tscT = Tile("scT")
    gqk10 = sb("gqk10", [128, 10, 64], F32)
    tg10 = Tile("gqk10")
    epsln = sb("epsln", [128, 1], F32)
    tepsln = Tile("epsln")
    negb = sb("negb", [128, 1], F32)
    tnegb = Tile("negb")
    tmp_small = sb("tmp_small", [128, 256], F32)
    ttmp_small = Tile("tmp_small")
    tedge = [Tile("edge%d" % i) for i in range(32)]
    tseam = [Tile("seam0"), Tile("seam1")]
    m2_sb = sb("m2_sb", [128, 512], F32)
    rstd_sb = sb("rstd_sb", [128, 512], F32)
    nmr_sb = sb("nmr_sb", [128, 512], F32)
    tm2, trstd, tnmr = Tile("m2"), Tile("rstd"), Tile("nmr")
    zb = [sb("zb%d" % i, [128, 512], BF16) for i in range(2)]
    zsq = [sb("zsq%d" % i, [128, 512], BF16) for i in range(2)]
    tzb = [Tile("zb%d" % i) for i in range(2)]
    tzsq = [Tile("zsq%d" % i) for i in range(2)]
    zrot = Rot(range(2))
    ntmp = [sb("ntmp%d" % i, [128, 512], F32) for i in range(2)]
    tntmp = [Tile("ntmp%d" % i) for i in range(2)]
    nrot = Rot(range(2))

    UW = 1032
    RR_N = 41824
    RR = sb("RR", [128, RR_N], BF16)
    actT = RR[:, 0:NHC * T].rearrange("p (c t) -> p c t", t=T)
    tact = [[Tile("act%d_%d" % (c, b)) for b in range(2)] for c in range(NHC)]
    cb0 = NHC * T
    convbuf = [RR[:, cb0 + i * UW * 2: cb0 + (i + 1) * UW * 2].bitcast(F32) for i in range(8)]
    tconv2 = [[Tile("cv%d_0" % i), Tile("cv%d_1" % i)] for i in range(8)]
    tconv = [None] * 8
    o = 0

    def carve(n_bf16):
        nonlocal o
        a = RR[:, o:o + n_bf16]
        o += n_bf16
        return a
    fS = carve(8 * 512).rearrange("p (i n) -> p i n", n=512)
    tfS = [Tile("fS%d" % i) for i in range(8)]
    qT = carve(4 * T).rearrange("p (j t) -> p j t", t=T)
    tqT = [Tile("qT%d" % i) for i in range(8)]
    KT = 1536
    kT = carve(4 * KT).rearrange("p (k v t) -> p k v t", v=2, t=KT)
    tkz = Tile("kTzero")
    tkT = [Tile("kT%d" % i) for i in range(12)]
    vaug = carve(12 * 2 * 192).rearrange("p (i k n) -> p i k n", k=2, n=192)
    tva = [Tile("va%d" % i) for i in range(12)]
    Gb = carve(2 * 4 * T).rearrange("p (s c t) -> p s c t", s=2, c=4)
    tG = [[[Tile("G%d%d%d" % (s, c, b)) for b in range(2)] for c in range(4)] for s in range(2)]
    pT2 = [carve(1024) for _ in range(2)]
    tpT2 = [Tile("pT2_%d" % i) for i in range(2)]
    sqbuf = carve(640 * 2).bitcast(F32)
    tsq = Tile("sqbuf")
    qn = [carve(640 * 2).bitcast(F32).rearrange("p (h d) -> p h d", d=64) for _ in range(2)]
    tqn = [Tile("qn0"), Tile("qn1")]
    rt1 = carve(640 * 2).bitcast(F32).rearrange("p (h d) -> p h d", d=64)
    rt2 = carve(640 * 2).bitcast(F32).rearrange("p (h d) -> p h d", d=64)
    trt1, trt2 = Tile("rt1"), Tile("rt2")
    q16k = [carve(6 * 128).rearrange("p (m n) -> p m n", n=128) for _ in range(2)]
    tq16 = [Tile("q16_0"), Tile("q16_1")]
    vst = [carve(128 * 2).bitcast(F32) for _ in range(2)]
    tvst = [Tile("vst0"), Tile("vst1")]
    msb = carve(16 * 2 * 3).bitcast(F32)
    tms = Tile("ms")
    lnrow = [carve(512 * 2).bitcast(F32) for _ in range(2)]
    tlnrow = [Tile("lnrow0"), Tile("lnrow1")]
    recb = [carve(512 * 2).bitcast(F32) for _ in range(2)]
    trecb = [Tile("recb0"), Tile("recb1")]
    assert o <= RR_N, (o, RR_N)
    mix_tiles = (tfS + tqT + tkT + [tkz] + tva + [t for a in tG for b_ in a for t in b_] + tpT2 + [tsq] + tqn +
                 [trt1, trt2] + tq16 + tvst + [tms] + tlnrow + trecb)
    ffn_tiles = [t for a in tact for t in a] + [t for a in tconv2 for t in a]

    def handoff(old, new):
        pend = {}
        for t in old:
            if t.last_w is not None:
                pend[id(t.last_w)] = t.last_w
            for r in t.readers:
                pend[id(r)] = r
        pl = list(pend.values())
        for t in new:
            t.readers = list(t.readers) + pl

    PSall = nc.alloc_psum_tensor("psall", [128, 8 * 512], F32)
    PS = [PSall[:, i * 512:(i + 1) * 512] for i in range(8)]
    tps = [Tile("ps%d" % i, excl=True) for i in range(8)]

    def MM(ps_ap, lhsT, rhs, start, stop, reads, writes):
        S.op("pe", lambda e: e.matmul(ps_ap, lhsT, rhs, start=start, stop=stop), reads, writes)

    def TR(ps_ap, in_ap, reads, writes):
        S.op("pe", lambda e: e.transpose(ps_ap, in_ap, identb), list(reads) + [tcbf], writes)

    def ACT(out, in_, func, reads, writes, scale=1.0, bias=0.0):
        S.op("act", lambda e: e.activation(out=out, in_=in_, func=func, scale=scale, bias=bias), reads, writes)

    def TT(out, in0, in1, op, reads, writes, eng="dve"):
        S.op(eng, lambda e: e.tensor_tensor(out=out, in0=in0, in1=in1, op=op), reads, writes)

    def STT(out, in0, scalar, in1, op0, op1, reads, writes):
        S.op("dve", lambda e: e.scalar_tensor_tensor(out=out, in0=in0, scalar=scalar, in1=in1, op0=op0, op1=op1),
             reads, writes)

    def TS(out, in0, s1, s2, op0, op1, reads, writes, eng="dve"):
        S.op(eng, lambda e: e.tensor_scalar(out=out, in0=in0, scalar1=s1, scalar2=s2, op0=op0, op1=op1), reads, writes)

    def CP(out, in_, reads, writes, eng="dve"):
        S.op(eng, lambda e: e.tensor_copy(out=out, in_=in_), reads, writes)

    def MS(ap, val, writes, eng="dve"):
        S.op(eng, lambda e: e.memset(ap, val), (), writes)

    def LD(out, in_, writes, q="sp"):
        S.dma(q, lambda e: e.dma_start(out=out, in_=in_), (), writes)

    def ST(out, in_, reads):
        S.dma("sp", lambda e: e.dma_start(out=out, in_=in_), reads, (), is_output=True)

    pinned = set()

    def wload(parts, pin=False):
        i = slot_rot.next()
        while i in pinned:
            i = slot_rot.next()
        if pin:
            pinned.add(i)
        for (view, nk, c0, ncols) in parts:
            tl = [tslot[i][0]] if c0 + ncols <= 256 else ([tslot[i][1]] if c0 >= 256 else tslot[i])
            src = view.rearrange("(k p) n -> p k n", p=128)
            dst = slots[i][:, 0:nk, c0:c0 + ncols]
            S.dma("pool", lambda e, dst=dst, src=src: e.dma_start(out=dst, in_=src), (), tl)
        return slots[i], tslot[i]

    def unpin(slot):
        for i_, sl_ in enumerate(slots):
            if sl_ is slot:
                pinned.discard(i_)

    def vslice(name, a=0, n=None):
        o0, nn = VOFF[name]
        n = nn - a if n is None else n
        return vec[:, o0 + a:o0 + a + n]

    def dvs(l, cnd, idx, c):
        return dv[:, l, cnd, idx, c:c + 1]

    A1S, A1B, G1, G2, LGA, LBA, LGB, LBB, LG1, LB1, LG2, LB2, TMPV = range(13)

    LD(vec[:, :], vecs_in, [tvec])
    LD(cbf[:, :], cbf_in, [tcbf])
    MS(onesf[:, :], 1.0, [tonesf])
    MS(epsln[:, :], EPS_LN, [tepsln])
    ACT(scT[:, :], vslice("cond"), AF.Silu, [tvec], [tscT])
    gq_b = vslice("gqk", 0, 64).unsqueeze(1).broadcast_to([128, 8, 64])
    gk_b = vslice("gqk", 64, 64).unsqueeze(1).broadcast_to([128, 2, 64])
    TS(gqk10[:, 0:8, :], gq_b, 0.125, None, ALU.mult, ALU.bypass, [tvec], [tg10])
    CP(gqk10[:, 8:10, :], gk_b, [tvec], [tg10])
    TS(tmp_small[:, 0:128], vslice("gqk"), -1.0, None, ALU.mult, ALU.bypass, [tvec], [ttmp_small])
    TT(tmp_small[:, 0:128], tmp_small[:, 0:128], vslice("gqk"), ALU.max, [tvec, ttmp_small], [ttmp_small])
    S.op("dve", lambda e: e.tensor_reduce(out=tmp_small[:, 128:130],
                                          in_=tmp_small[:, 0:128].rearrange("p (a b) -> p a b", b=64),
                                          op=ALU.max, axis=AX.X), [ttmp_small], [ttmp_small])
    STT(negb[:, :], tmp_small[:, 128:129], -8.0, tmp_small[:, 129:130], ALU.mult, ALU.mult, [ttmp_small], [tnegb])

    bg_jobs = [(l_, s_) for l_ in range(2) for s_ in range(12)]
    bg_state = {"loaded": [], "next": 0, "done": 0}
    BG_DEPTH = 2

    def bg_issue(force=False):
        if not (FLAG_BG or force):
            return
        while len(bg_state["loaded"]) < (BG_DEPTH if FLAG_BG else 1) and bg_state["next"] < len(bg_jobs):
            l, sj = bg_jobs[bg_state["next"]]
            bg_state["next"] += 1
            slot, tl = wload([(w_ada[l][:, sj * 512:(sj + 1) * 512], 8, 0, 512)], pin=True)
            bg_state["loaded"].append((l, sj, slot, tl))

    def bg_compute(bank=None):
        if not bg_state["loaded"]:
            return
        l, sj, slot, tl = bg_state["loaded"].pop(0)
        pi = mainrot.next() if bank is None else bank
        psm = PS[pi]
        for jj in range(4):
            for k in range(8):
                MM(psm[:, 2 * jj:2 * jj + 2], slot[:, k, jj * 128:(jj + 1) * 128], scT[:, 2 * k:2 * k + 2],
                   k == 0, k == 7, tl + [tscT], [tps[pi]])
        for i_, sl_ in enumerate(slots):
            if sl_ is slot:
                pinned.discard(i_)
        j0 = 4 * sj
        bb = vslice("bada", l * 48 + j0, 4).unsqueeze(2).broadcast_to([128, 4, 2])
        TT(mod_sb[:, l, 2 * j0:2 * j0 + 8].rearrange("p (j c) -> p j c", c=2),
           psm[:, 0:8].rearrange("p (j c) -> p j c", c=2), bb, ALU.add, [tps[pi], tvec], [tmod[l]])
        bg_state["done"] += 1

    def bg_step(bank=None):
        if not FLAG_BG:
            return
        bg_compute(bank)
        bg_issue()

    def bg_until(n):
        while bg_state["done"] < n:
            bg_issue(force=True)
            bg_compute()

    def derive_in(l):
        for cnd in range(2):
            m = mod_sb[:, l, :].rearrange("p (j c) -> p j c", c=2)
            d = dv[:, l, cnd]
            rd = [tmod[l], tdv[l][cnd]]
            wr = [tdv[l][cnd]]
            TS(d[:, A1S, :], m[:, 8:16, cnd], 1.0, None, ALU.add, ALU.bypass, rd, wr)
            CP(d[:, A1B, :], m[:, 0:8, cnd], rd, wr)

    def derive(l):
        for cnd in range(2):
            m = mod_sb[:, l, :].rearrange("p (j c) -> p j c", c=2)

            def mv(k):
                return m[:, 8 * k:8 * k + 8, cnd]
            d = dv[:, l, cnd]
            rd = [tmod[l], tvec, tdv[l][cnd]]
            wr = [tdv[l][cnd]]
            lg1 = vslice("lng", (l * 2 + 0) * 8, 8)
            lb1 = vslice("lnb", (l * 2 + 0) * 8, 8)
            lg2 = vslice("lng", (l * 2 + 1) * 8, 8)
            lb2 = vslice("lnb", (l * 2 + 1) * 8, 8)
            TS(d[:, G1, :], mv(2), 1.0 / ALPHA, None, ALU.mult, ALU.bypass, rd, wr)
            TS(d[:, G2, :], mv(5), 1.0 / ALPHA, None, ALU.mult, ALU.bypass, rd, wr)
            TS(d[:, TMPV, :], mv(4), 1.0, None, ALU.add, ALU.bypass, rd, wr)
            TT(d[:, LGA, :], lg1, d[:, TMPV, :], ALU.mult, rd, wr)
            TT(d[:, LBA, :], lb1, d[:, TMPV, :], ALU.mult, rd, wr)
            TT(d[:, LBA, :], d[:, LBA, :], mv(3), ALU.add, rd, wr)
            CP(d[:, LG1, :], lg1, rd, wr)
            CP(d[:, LB1, :], lb1, rd, wr)
            CP(d[:, LG2, :], lg2, rd, wr)
            CP(d[:, LB2, :], lb2, rd, wr)

    def derive_cross():
        for cnd in range(2):
            d0 = dv[:, 0, cnd]
            d1 = dv[:, 1, cnd]
            rd = [tdv[0][cnd], tdv[1][cnd]]
            wr = [tdv[0][cnd]]
            TT(d0[:, LGB, :], d0[:, LG2, :], d1[:, A1S, :], ALU.mult, rd, wr)
            TT(d0[:, LBB, :], d0[:, LB2, :], d1[:, A1S, :], ALU.mult, rd, wr)
            TT(d0[:, LBB, :], d0[:, LBB, :], d1[:, A1B, :], ALU.add, rd, wr)

    mainrot = Rot([0, 1, 2, 3])
    rot8 = Rot(range(8))

    class LNState:
        pass

    def ln_begin():
        st = LNState()
        st.cnt = [0, 0]
        st.pending = []
        return st

    def ln_accum(st, l, cnd, gate_idx, d, blk, ps_i, pool_sq=False):
        xs = xT[:, d, blk * 512:(blk + 1) * 512]
        STT(xs, PS[ps_i][:, :], dvs(l, cnd, gate_idx, d), xs, ALU.mult, ALU.add,
            [tps[ps_i], tx[d][blk], tdv[l][cnd]], [tx[d][blk]])
        zi = zrot.next()
        ACT(zb[zi][:, :], xs, AF.Copy, [tx[d][blk]], [tzb[zi]])
        if pool_sq:
            TT(zsq[zi][:, :], xs, xs, ALU.mult, [tx[d][blk]], [tzsq[zi]], eng="pool")
        else:
            ACT(zsq[zi][:, :], xs, AF.Square, [tx[d][blk]], [tzsq[zi]])
        st.pending.append((blk, zi))
        if len(st.pending) > 1:
            ln_flush_one(st)

    def ln_flush_one(st):
        blk, zi = st.pending.pop(0)
        first = st.cnt[blk] == 0
        last = st.cnt[blk] == 7
        st.cnt[blk] += 1
        MM(PS[4 + 2 * blk][:, :], onesD, zb[zi][:, :], first, last, [tzb[zi], tcbf], [tps[4 + 2 * blk]])
        MM(PS[5 + 2 * blk][:, :], onesD, zsq[zi][:, :], first, last, [tzsq[zi], tcbf], [tps[5 + 2 * blk]])

    def ln_fin_stats(st, blk):
        psum_, pssq_ = 4 + 2 * blk, 5 + 2 * blk
        ACT(m2_sb[:, :], PS[psum_][:, :], AF.Square, [tps[psum_]], [tm2])
        STT(m2_sb[:, :], m2_sb[:, :], -1.0, PS[pssq_][:, :], ALU.mult, ALU.add, [tps[pssq_], tm2], [tm2])
        ACT(rstd_sb[:, :], m2_sb[:, :], AF.Ln, [tm2, tepsln], [trstd], bias=epsln[:, 0:1])
        ACT(rstd_sb[:, :], rstd_sb[:, :], AF.Exp, [trstd], [trstd], scale=-0.5)
        STT(nmr_sb[:, :], PS[psum_][:, :], -1.0, rstd_sb[:, :], ALU.mult, ALU.mult, [tps[psum_], trstd], [tnmr])

    def ln_fin_chunk(st, l, cnd, blk, d, g_idx, b_idx, hg_idx, hb_idx):
        xs = xT[:, d, blk * 512:(blk + 1) * 512]
        ni = nrot.next()
        TT(ntmp[ni][:, :], xs, rstd_sb[:, :], ALU.mult, [tx[d][blk], trstd], [tntmp[ni]])
        TT(ntmp[ni][:, :], ntmp[ni][:, :], nmr_sb[:, :], ALU.add, [tntmp[ni], tnmr], [tntmp[ni]])
        ACT(xs, ntmp[ni][:, :], AF.Identity, [tntmp[ni], tdv[l][cnd]], [tx[d][blk]],
            scale=dvs(l, cnd, g_idx, d), bias=dvs(l, cnd, b_idx, d))
        if hg_idx is not None:
            ACT(actA[:, d, blk * 512:(blk + 1) * 512], ntmp[ni][:, :], AF.Identity, [tntmp[ni], tdv[l][cnd]],
                [tA[d][blk]], scale=dvs(l, cnd, hg_idx, d), bias=dvs(l, cnd, hb_idx, d))

    def ln_finalize(st, l, cnd, blk, g_idx, b_idx, hg_idx, hb_idx, flush=True):
        while flush and st.pending:
            ln_flush_one(st)
        ln_fin_stats(st, blk)
        for d in range(8):
            ln_fin_chunk(st, l, cnd, blk, d, g_idx, b_idx, hg_idx, hb_idx)

    prefetched_up = {}
    wdown_pref = {}

    def up_slot_parts(l, s_):
        return [(w_up[l][:, s_ * 256:(s_ + 1) * 256], 8, 0, 256),
                (w_up[l][:, DFF + s_ * 256:DFF + (s_ + 1) * 256], 8, 256, 256)]

    def outproj_ln(l, cnd, wd, nK, rhs_fn, gate_idx, g_idx, b_idx, hg_idx, hb_idx):
        st = ln_begin()
        while bg_state["loaded"]:
            bg_compute()
        if nK == 8:
            prefetched_up[(l, cnd)] = {s_: wload(up_slot_parts(l, s_), pin=True) for s_ in range(2)}
        kgroups = [(k0, min(8, nK - k0)) for k0 in range(0, nK, 8)]
        sl = {}
        pre = wdown_pref.pop((l, cnd), []) if nK == NHC else []
        for ch in range(2):
            sl[ch] = []
            for gi, (k0, nk) in enumerate(kgroups):
                if ch == 0 and gi < len(pre):
                    sl[ch].append(pre[gi])
                else:
                    sl[ch].append(wload([(wd[k0 * 128:(k0 + nk) * 128, ch * 512:(ch + 1) * 512], nk, 0, 512)], pin=True))
        for ch in range(2):
            for w_ in sl[ch]:
                unpin(w_[0])
        fin = [False, False]
        fin_next = 8
        if not FLAG_BLKOUTER:
            order = [(blk, ch * 4 + dc) for ch in range(2) for blk in range(2) for dc in range(4)]
        else:
            order = [(blk, d) for blk in range(2) for d in range(8)]
        for (blk, d) in order:
            if True:
                ch, dc = d // 4, d % 4
                pi = mainrot.next()
                for kc in range(nK):
                    slot, tl = sl[ch][kc // 8]
                    rap, rt = rhs_fn(kc, blk)
                    MM(PS[pi][:, :], slot[:, kc % 8, dc * 128:(dc + 1) * 128], rap, kc == 0, kc == nK - 1,
                       tl + [rt], [tps[pi]])
                ln_accum(st, l, cnd, gate_idx, d, blk, pi, pool_sq=(nK == 8))
                if FLAG_BLKOUTER and blk == 1 and st.cnt[0] == 8 and not fin[0]:
                    fin[0] = True
                    ln_fin_stats(st, 0)
                    fin_next = 0
                elif FLAG_BLKOUTER and blk == 1 and fin[0] and fin_next < 8:
                    for _ in range(2):
                        ln_fin_chunk(st, l, cnd, 0, fin_next, g_idx, b_idx, hg_idx, hb_idx)
                        fin_next += 1
        if not fin[0]:
            ln_finalize(st, l, cnd, 0, g_idx, b_idx, hg_idx, hb_idx)
        else:
            while fin_next < 8:
                ln_fin_chunk(st, l, cnd, 0, fin_next, g_idx, b_idx, hg_idx, hb_idx)
                fin_next += 1
        ln_finalize(st, l, cnd, 1, g_idx, b_idx, hg_idx, hb_idx)

    def seg_view(buf, h, blk):
        if h == 0:
            c0 = 1 + 258 * (2 * blk)
            return buf[:, c0:c0 + 2 * 258 - 1].rearrange("p (s w) -> p s w", w=258)[:, :, 0:256] if False else \
                buf[:, 258 * 2 * blk:258 * 2 * (blk + 1)].rearrange("p (s w) -> p s w", w=258)[:, :, 1:257]
        return buf[:, 1 + 512 * blk:1 + 512 * (blk + 1)]

    def ps_view(ap, h):
        if h == 0:
            return ap.rearrange("p (s w) -> p s w", w=256)
        return ap

    def tok_view(buf, h):
        if h == 0:
            return buf[:, 0:4 * 258].rearrange("p (s w) -> p s w", w=258)[:, :, 1:257]
        return buf[:, 1:1 + T]

    def tok_flat(ap, h):
        if h == 0:
            return ap.rearrange("p (s w) -> p s w", w=256)
        return ap

    def conv_width(h):
        return 4 * 258 - 2 if h == 0 else T

    def conv3(ubuf, tu, ybuf, ty, w0, w1, w2, wt, h):
        W = conv_width(h)
        ACT(ybuf[:, 1:1 + W], ubuf[:, 1:1 + W], AF.Identity, tu + [wt], ty, scale=w1)
        STT(ybuf[:, 1:1 + W], ubuf[:, 0:W], w0, ybuf[:, 1:1 + W], ALU.mult, ALU.add, tu + ty + [wt], ty)
        STT(ybuf[:, 1:1 + W], ubuf[:, 2:2 + W], w2, ybuf[:, 1:1 + W], ALU.mult, ALU.add, tu + ty + [wt], ty)

    def load_x(h):
        for c in range(8):
            S.dma("sp", lambda e, c=c: e.dma_start(out=xT[:, c, :], in_=x_in[:, c, h * T:(h + 1) * T]),
                  (), [tx[c][0], tx[c][1]])

    def store_y(h):
        for c in range(8):
            S.dma("sp", lambda e, c=c: e.dma_start(out=y_out[:, c, h * T:(h + 1) * T], in_=xT[:, c, :]),
                  [tx[c][0], tx[c][1]], (), is_output=True)

    def modulate_in(l, cnd):
        for c in range(8):
            for blk in range(2):
                ACT(actA[:, c, blk * 512:(blk + 1) * 512], xT[:, c, blk * 512:(blk + 1) * 512], AF.Identity,
                    [tx[c][blk], tdv[l][cnd]], [tA[c][blk]], scale=dvs(l, cnd, A1S, c), bias=dvs(l, cnd, A1B, c))

    def mixer_even(h):
        cnd = h
        MS(vaug.rearrange("p i k n -> p (i k n)").bitcast(F32), 0.0, tva)
        MS(vaug.rearrange("p i k n -> p (i k) n")[:, :, 64:65], 1.0, tva)
        nkt = 8 if h == 0 else 12
        kflat = kT.rearrange("p k v t -> p (k v t)").bitcast(F32)
        MS(kflat, 0.0, [tkz] + tkT)
        wf = wload([(w_in_a[0][:, 0:512], 8, 0, 512)], pin=True)
        wq = wload([(w_in_a[0][:, 512:1024], 8, 0, 512)], pin=True)
        wkv = wload([(w_in_a[0][:, 1024:1280], 8, 0, 256)], pin=True)
        if h == 0:
            bg_issue()
        if h == 1:
            S.dma("pool", lambda e: e.dma_start(out=kT[0:64, :, 0, 1024:1536], in_=ck_in[0:64]), (), tkT[8:12])
            S.dma("pool", lambda e: e.dma_start(out=kT[64:128, :, 1, 1024:1536], in_=ck_in[64:128]), (), tkT[8:12])
            cvv = cv_in.rearrange("p t (k d) -> p t k d", d=64)
            S.dma("pool", lambda e: e.dma_start(out=vaug[:, 8:12, :, 0:64], in_=cvv), (), tva[8:12])
            S.dma("pool", lambda e: e.dma_start(out=vaug[:, 8:12, :, 128:192], in_=cvv), (), tva[8:12])
        prot3 = Rot([(0, 1, 2), (3, 4, 5)])
        trot = Rot([6, 7])
        pend_tr = []
        for i in range(8):
            blk = i // 4
            pf, pq, pkv = prot3.next()
            for (pi, (slot, tl), ncol) in ((pf, wf, 512), (pq, wq, 512), (pkv, wkv, 256)):
                for k in range(8):
                    MM(PS[pi][:, 0:ncol], actA[:, k, i * 128:(i + 1) * 128], slot[:, k, 0:ncol], k == 0, k == 7,
                       tl + [tA[k][blk]], [tps[pi]])
            while pend_tr:
                pend_tr.pop(0)()
            b2 = i % 2
            ACT(fS[:, i, :], PS[pf][:, :], AF.Copy, [tps[pf]], [tfS[i]])
            ACT(sqbuf[:, 0:512], PS[pq][:, :], AF.Square, [tps[pq]], [tsq])
            ACT(sqbuf[:, 512:640], PS[pkv][:, 0:128], AF.Square, [tps[pkv]], [tsq])
            S.op("dve", lambda e: e.tensor_reduce(out=msb[:, 0:10], in_=sqbuf[:, :].rearrange("p (h d) -> p h d", d=64),
                                                  op=ALU.add, axis=AX.X), [tsq], [tms])
            TS(msb[:, 0:10], msb[:, 0:10], 1.0 / 64.0, EPS, ALU.mult, ALU.add, [tms], [tms])
            ACT(msb[:, 16:26], msb[:, 0:10], AF.Ln, [tms], [tms])
            ACT(msb[:, 32:42], msb[:, 16:26], AF.Exp, [tms], [tms], scale=-0.5)
            q_ = qn[b2]
            TT(q_[:, 0:8, :], PS[pq][:, :].rearrange("p (h d) -> p h d", d=64),
               msb[:, 32:40].unsqueeze(2).broadcast_to([128, 8, 64]), ALU.mult, [tps[pq], tms], [tqn[b2]])
            TT(q_[:, 8:10, :], PS[pkv][:, 0:128].rearrange("p (h d) -> p h d", d=64),
               msb[:, 40:42].unsqueeze(2).broadcast_to([128, 2, 64]), ALU.mult, [tps[pkv], tms], [tqn[b2]])
            TT(q_[:, :, :], q_[:, :, :], gqk10[:, :, :], ALU.mult, [tqn[b2], tg10], [tqn[b2]])
            if h == 0:
                S.dma("sp", lambda e, q_=q_, i=i: e.dma_start(
                    out=nk_out[i * 128:(i + 1) * 128, :], in_=q_[:, 8:10, :].rearrange("p h d -> p (h d)")),
                    [tqn[b2]], (), is_output=True)
                ACT(vst[b2][:, :], PS[pkv][:, 128:256], AF.Copy, [tps[pkv]], [tvst[b2]])
                S.dma("sp", lambda e, b2=b2, i=i: e.dma_start(out=nv_out[i * 128:(i + 1) * 128, :], in_=vst[b2][:, :]),
                      [tvst[b2]], (), is_output=True)
            else:
                cosb = vslice("cosF", i * 64, 64).unsqueeze(1).broadcast_to([128, 10, 64])
                TT(rt1[:, :, :], q_[:, :, :], cosb, ALU.mult, [tqn[b2], tvec], [trt1])
                for g in range(2):
                    lo = slice(g * 32, g * 32 + 16)
                    hi = slice(g * 32 + 16, g * 32 + 32)
                    s_lo = vslice("sinF", i * 64 + g * 32, 16).unsqueeze(1).broadcast_to([128, 10, 16])
                    s_hi = vslice("sinF", i * 64 + g * 32 + 16, 16).unsqueeze(1).broadcast_to([128, 10, 16])
                    TT(rt2[:, :, lo], q_[:, :, hi], s_lo, ALU.mult, [tqn[b2], tvec], [trt2])
                    TT(rt2[:, :, hi], q_[:, :, lo], s_hi, ALU.mult, [tqn[b2], tvec], [trt2])
                TT(q_[:, :, :], rt1[:, :, :], rt2[:, :, :], ALU.add, [trt1, trt2], [tqn[b2]])
            ACT(q16k[b2][:, 0:4, :], q_[:, 0:8, :].rearrange("p (j a) d -> p j (a d)", a=2), AF.Copy,
                [tqn[b2]], [tq16[b2]])
            for kvh in range(2):
                CP(q16k[b2][:, 4 + kvh, :].rearrange("p (a d) -> p a d", d=64),
                   q_[:, 8 + kvh, :].unsqueeze(1).broadcast_to([128, 2, 64]), [tqn[b2]], [tq16[b2]])
            vsrc = PS[pkv][:, 128:256].rearrange("p (k d) -> p k d", d=64)
            ACT(vaug[:, i, :, 0:64], vsrc, AF.Copy, [tps[pkv]], [tva[i]])
            ACT(vaug[:, i, :, 128:192], vsrc, AF.Copy, [tps[pkv]], [tva[i]])
            def do_tr(i=i, b2=b2):
                ti = trot.next()
                ptr = PS[ti][:, :].bitcast(BF16)[:, 0:768].rearrange("p (m n) -> p m n", n=128)
                for m in range(6):
                    TR(ptr[:, m, :], q16k[b2][:, m, :], [tq16[b2]], [tps[ti]])
                CP(qT[:, :, i * 128:(i + 1) * 128], ptr[:, 0:4, :], [tps[ti]], [tqT[i]])
                CP(kT[0:64, :, 0, i * 128:(i + 1) * 128], ptr[0:64, 4:6, :], [tps[ti]], [tkT[i]])
                CP(kT[64:128, :, 1, i * 128:(i + 1) * 128], ptr[64:128, 4:6, :], [tps[ti]], [tkT[i]])
            if FLAG_DEFER_TR:
                pend_tr.append(do_tr)
            else:
                do_tr()
            if h == 0 and i % 2 == 1:
                bg_step()
        while pend_tr:
            pend_tr.pop(0)()
        for w_ in (wf, wq, wkv):
            unpin(w_[0])

        if h == 0:
            nseq, Sq, nst = 4, 256, 2
            loads = [(cs, 0) for cs in range(2)]
        else:
            nseq, Sq, nst = 1, 1024, 8
            loads = [(cs, sbk) for cs in range(2) for sbk in range(2)]
        N1 = min(Sq, 512)
        for (cs, sbk) in loads:
            if h == 0:
                slot, tl = wload([(dftp_in[cs][:, :], 2, 0, 256)])
            else:
                slot, tl = wload([(dfts_in[cs][:, sbk * 512:(sbk + 1) * 512], 8, 0, 512)])
            for sq_ in range(nseq):
                for c in range(4):
                    pi = mainrot.next()
                    for st_ in range(nst):
                        ti_ = sq_ * nst + st_
                        MM(PS[pi][:, 0:N1], fS[:, ti_, c * 128:(c + 1) * 128], slot[:, st_, 0:N1], st_ == 0,
                           st_ == nst - 1, tl + [tfS[ti_]], [tps[pi]])
                    t0 = sq_ * Sq + sbk * 512
                    blk = t0 // 512
                    ACT(Gb[:, cs, c, t0:t0 + N1], PS[pi][:, 0:N1], AF.Copy, [tps[pi]], [tG[cs][c][blk]])
        for c in range(4):
            for blk in range(2):
                pi = mainrot.next()
                MM(PS[pi][:, :], BDc, Gb[:, 0, c, blk * 512:(blk + 1) * 512], True, False, [tcbf, tG[0][c][blk]], [tps[pi]])
                MM(PS[pi][:, :], BDsn, Gb[:, 1, c, blk * 512:(blk + 1) * 512], False, True, [tcbf, tG[1][c][blk]], [tps[pi]])
                CP(actA[:, c, blk * 512:(blk + 1) * 512], PS[pi][:, :], [tps[pi]], [tA[c][blk]])

        if h == 0:
            srot = Rot([(0,), (1,), (2,)])
        else:
            srot = Rot([(0, 1), (2, 3)])
        orot = Rot([4, 5])
        brot = Rot([6, 7])
        lrot = Rot([0, 1])
        p2rot = Rot([0, 1])
        pending_norm = []
        if h == 0:
            jobs = [(sq_ * 256, 256, [2 * sq_, 2 * sq_ + 1]) for sq_ in range(4)]
        else:
            jobs = [(qb * 512, 512, list(range(12))) for qb in range(2)]
        steps = []
        for (q0, Nq, ktiles) in jobs:
            for hd in range(8):
                hj = {"q0": q0, "Nq": Nq, "ktiles": ktiles, "hd": hd, "kvh": hd // 4, "j": hd // 2,
                      "base": (hd % 2) * 64, "npair": len(ktiles) // 2, "po": None, "blk": q0 // 512,
                      "qtl": [tqT[t] for t in range(q0 // 128, (q0 + Nq) // 128)]}
                for pi_ in range(hj["npair"]):
                    steps.append({"hj": hj, "pi": pi_, "bks": None})

        def qk2(st_):
            hj = st_["hj"]
            bks = srot.next()
            st_["bks"] = bks
            for u_ in range(2):
                kt = hj["ktiles"][2 * st_["pi"] + u_]
                if h == 0:
                    dst = PS[bks[0]][:, u_ * 256:(u_ + 1) * 256]
                    tb = tps[bks[0]]
                else:
                    dst = PS[bks[u_]][:, :]
                    tb = tps[bks[u_]]
                MM(dst, kT[:, hj["kvh"], hj["hd"] % 2, kt * 128:(kt + 1) * 128],
                   qT[:, hj["j"], hj["q0"]:hj["q0"] + hj["Nq"]], True, True, [tkT[kt], tkz] + hj["qtl"], [tb])

        def pv2(st_):
            hj = st_["hj"]
            Nq = hj["Nq"]
            if st_["pi"] == 0:
                hj["po"] = orot.next()
            po = hj["po"]
            bks = st_["bks"]
            pb_ = p2rot.next()
            if h == 0:
                src = PS[bks[0]][:, :]
            else:
                src = PSall[:, bks[0] * 512:(bks[0] + 2) * 512]
            ACT(pT2[pb_][:, 0:2 * Nq], src, AF.Exp, [tps[b_] for b_ in bks] + [tnegb], [tpT2[pb_]],
                bias=negb[:, 0:1])
            for u_ in range(2):
                ci = 2 * st_["pi"] + u_
                kt = hj["ktiles"][ci]
                rhs_ = pT2[pb_][:, u_ * Nq:(u_ + 1) * Nq]
                if hj["base"] == 0:
                    MM(PS[po][0:65, 0:Nq], vaug[:, kt, hj["kvh"], 0:65], rhs_, ci == 0, ci == 2 * hj["npair"] - 1,
                       [tva[kt], tpT2[pb_]], [tps[po]])
                else:
                    MM(PS[po][:, 0:Nq], vaug[:, kt, hj["kvh"], 64:192], rhs_, ci == 0, ci == 2 * hj["npair"] - 1,
                       [tva[kt], tpT2[pb_]], [tps[po]])

        def make_norm(hj):
            po, base, j, q0, Nq, blk = hj["po"], hj["base"], hj["j"], hj["q0"], hj["Nq"], hj["blk"]

            def norm():
                rp = 64 if base == 0 else 0
                li = lrot.next()
                ACT(lnrow[li][rp:rp + 1, 0:Nq], PS[po][rp:rp + 1, 0:Nq], AF.Ln, [tps[po]], [tlnrow[li]])
                bi = brot.next()
                if base == 0:
                    MM(PS[bi][0:64, 0:Nq], onesf[64:65, 0:64], lnrow[li][64:65, 0:Nq], True, True,
                       [tonesf, tlnrow[li]], [tps[bi]])
                else:
                    MM(PS[bi][:, 0:Nq], onesf[0:1, :], lnrow[li][0:1, 0:Nq], True, True,
                       [tonesf, tlnrow[li]], [tps[bi]])
                ACT(recb[li][base:base + 64, 0:Nq], PS[bi][base:base + 64, 0:Nq], AF.Exp, [tps[bi]], [trecb[li]],
                    scale=-1.0)
                TT(actA[base:base + 64, 4 + j, q0:q0 + Nq], PS[po][base:base + 64, 0:Nq],
                   recb[li][base:base + 64, 0:Nq], ALU.mult, [tps[po], trecb[li]], [tA[4 + j][blk]])
            return norm

        qk2(steps[0])
        for si_, st_ in enumerate(steps):
            if si_ + 1 < len(steps):
                qk2(steps[si_ + 1])
            pv2(st_)
            while pending_norm:
                pending_norm.pop(0)()
            hj = st_["hj"]
            if st_["pi"] == hj["npair"] - 1:
                pending_norm.append(make_norm(hj))
                if h == 0 and hj["hd"] in (3, 7):
                    bg_step(bank=3)
        while pending_norm:
            pending_norm.pop(0)()

        if h == 0:
            bg_until(12)
            derive(0)
            bg_issue()
        outproj_ln(0, cnd, w_out_a[0], 8, lambda kc, blk: (actA[:, kc, blk * 512:(blk + 1) * 512], tA[kc][blk]),
                   G1, LG1, LB1, LGA, LBA)

    def mixer_conv(h):
        cnd = h
        l = 1
        for i in range(6):
            MS(convbuf[i][:, :], 0.0, tconv2[i])

        def load_c(c):
            return wload([(w_in_c[0][:, c * 128:(c + 1) * 128], 8, 0, 128),
                          (w_in_c[0][:, D + c * 128:D + (c + 1) * 128], 8, 128, 128),
                          (w_in_c[0][:, 2 * D + c * 128:2 * D + (c + 1) * 128], 8, 256, 128)], pin=True)
        order = [(0, 0), (1, 0), (0, 1), (1, 1)] + [(c_, b_) for c_ in range(2, 8) for b_ in range(2)]
        loaded = {}

        def ensure(c_):
            if c_ < 8 and c_ not in loaded:
                loaded[c_] = load_c(c_)
        ensure(0)
        ensure(1)
        for (c, blk) in order:
            ensure(c)
            ensure(c + 1)
            ensure(c + 2)
            slot, tl = loaded[c]
            s3 = 3 * (c % 2)
            ub, yb, bgb = convbuf[s3], convbuf[s3 + 1], convbuf[s3 + 2]
            tu, ty, tbg = tconv2[s3], tconv2[s3 + 1], tconv2[s3 + 2]
            pis = []
            for part in range(3):
                pi = rot8.next()
                pis.append(pi)
                for k in range(8):
                    MM(PS[pi][:, :], slot[:, k, part * 128:(part + 1) * 128], actA[:, k, blk * 512:(blk + 1) * 512],
                       k == 0, k == 7, tl + [tA[k][blk]], [tps[pi]])
            ACT(seg_view(bgb, h, blk), ps_view(PS[pis[0]][:, :], h), AF.Copy, [tps[pis[0]]], [tbg[blk]])
            ACT(seg_view(yb, h, blk), ps_view(PS[pis[2]][:, :], h), AF.Copy, [tps[pis[2]]], [ty[blk]])
            TT(seg_view(ub, h, blk), ps_view(PS[pis[1]][:, :], h), seg_view(yb, h, blk), ALU.mult,
               [tps[pis[1]], ty[blk]], [tu[blk]])
            if blk == 1:
                unpin(slot)
                wv = [vec[:, VOFF["convc"][0] + k * 8 + c: VOFF["convc"][0] + k * 8 + c + 1] for k in range(3)]
                conv3(ub, tu, yb, ty, wv[0], wv[1], wv[2], tvec, h)
                TT(tok_flat(actT[:, c, :], h), tok_view(yb, h), tok_view(bgb, h), ALU.mult, ty + tbg,
                   [tact[c][0], tact[c][1]])
        outproj_ln(1, cnd, w_out_c[0], 8, lambda kc, blk: (actT[:, kc, blk * 512:(blk + 1) * 512], tact[kc][blk]),
                   G1, LG1, LB1, LGA, LBA)

    def ffn(l, h, last):
        cnd = h
        o0 = VOFF["convf"][0] + l * 132
        yrot = Rot(range(4))
        pending_tail = []

        def run_tail():
            while pending_tail:
                pending_tail.pop(0)()

        def taps_blk(y, ty2, banks, w0, w2, blk):
            P = PS[banks[blk]]
            rd = [tps[banks[blk]], ty2[blk], tvec]
            wr = [ty2[blk]]
            if h == 0:
                P3 = P[:, :].rearrange("p (s w) -> p s w", w=256)
                Y3 = y[:, blk * 512:(blk + 1) * 512].rearrange("p (s w) -> p s w", w=256)
                STT(Y3[:, :, 1:256], P3[:, :, 0:255], w0, Y3[:, :, 1:256], ALU.mult, ALU.add, rd, wr)
                STT(Y3[:, :, 0:255], P3[:, :, 1:256], w2, Y3[:, :, 0:255], ALU.mult, ALU.add, rd, wr)
            else:
                b0 = blk * 512
                STT(y[:, b0 + 1:b0 + 512], P[:, 0:511], w0, y[:, b0 + 1:b0 + 512], ALU.mult, ALU.add, rd, wr)
                STT(y[:, b0:b0 + 511], P[:, 1:512], w2, y[:, b0:b0 + 511], ALU.mult, ALU.add, rd, wr)

        erot = Rot(range(8))
        edge_idx = {}
        RRf = RR[:, cb0:cb0 + 8 * UW * 2].bitcast(F32)
        seam_rot = Rot(range(2))

        def quad(ap, n, col):
            return ap.rearrange("p (j a n) -> p a j n", j=2, a=2, n=n)[:, :, :, col]

        def seam_save(s_, b0):
            ei = erot.next()
            edge_idx[s_] = ei
            dst = tmp_small[:, 160 + 4 * ei:164 + 4 * ei].rearrange("p (a j) -> p a j", j=2)
            CP(dst, quad(PSall[:, b0 * 512:(b0 + 4) * 512], 512, 511), [tps[b0 + i_] for i_ in range(4)], [tedge[ei]])

        def seam_apply(s_, b1, ys0, c0, info_):
            ei = edge_idx[s_]
            E = tmp_small[:, 160 + 4 * ei:164 + 4 * ei].rearrange("p (a j) -> p a j", j=2)
            W0 = vec[:, o0 + c0:o0 + c0 + 44].rearrange("p (a x) -> p a x", x=22)[:, :, 0:2]
            W2 = vec[:, o0 + 88 + c0:o0 + 88 + c0 + 44].rearrange("p (a x) -> p a x", x=22)[:, :, 0:2]
            Yq = RRf[:, 2 * ys0 * UW:(2 * ys0 + 4) * UW]
            Y512, Y511 = quad(Yq, UW, 512), quad(Yq, UW, 511)
            P10 = quad(PSall[:, b1 * 512:(b1 + 4) * 512], 512, 0)
            k_ = seam_rot.next()
            tA_ = tmp_small[:, 224 + 8 * k_:228 + 8 * k_].rearrange("p (a j) -> p a j", j=2)
            tB_ = tmp_small[:, 228 + 8 * k_:232 + 8 * k_].rearrange("p (a j) -> p a j", j=2)
            ty1 = [info_[(jj_, pt_)][1][1] for jj_ in range(2) for pt_ in range(2)]
            ty0 = [info_[(jj_, pt_)][1][0] for jj_ in range(2) for pt_ in range(2)]
            TT(tA_, E, W0, ALU.mult, [tedge[ei], tvec, tseam[k_]], [tseam[k_]])
            TT(Y512, Y512, tA_, ALU.add, [tseam[k_]] + ty1, ty1)
            TT(tB_, P10, W2, ALU.mult, [tps[b1 + i_] for i_ in range(4)] + [tvec, tseam[k_]], [tseam[k_]])
            TT(Y511, Y511, tB_, ALU.add, [tseam[k_]] + ty0, ty0)

        def load_slot(s):
            return wload([(w_up[l][:, s * 256:(s + 1) * 256], 8, 0, 256),
                          (w_up[l][:, DFF + s * 256:DFF + (s + 1) * 256], 8, 256, 256)], pin=True)
        order = [(0, 0), (1, 0), (0, 1), (1, 1)] + [(s_, b_) for s_ in range(2, 11) for b_ in range(2)]
        loaded = {}
        infos = {}
        bankmap = {}

        def ensure(s_):
            if s_ < 11 and s_ not in loaded:
                pf = prefetched_up.get((l, cnd), {})
                if s_ in pf:
                    loaded[s_] = pf.pop(s_)
                else:
                    loaded[s_] = load_slot(s_)
        ensure(0)
        ensure(1)
        for (s, blk) in order:
            ensure(s)
            ensure(s + 1)
            if not (l == 0 and h == 0) and s < 8:
                ensure(s + 2)
            slot, tl = loaded[s]
            if s not in infos:
                info = {}
                for jj in range(2):
                    c = 2 * s + jj
                    ys = yrot.next()
                    if jj == 0:
                        info["ys0"] = ys
                    wa = [vec[:, o0 + k * 44 + c: o0 + k * 44 + c + 1] for k in range(3)]
                    wg = [vec[:, o0 + k * 44 + 22 + c: o0 + k * 44 + 22 + c + 1] for k in range(3)]
                    info[(jj, 0)] = (convbuf[2 * ys], tconv2[2 * ys], wa)
                    info[(jj, 1)] = (convbuf[2 * ys + 1], tconv2[2 * ys + 1], wg)
                infos[s] = info
            info = infos[s]
            for jj in range(2):
                for part in range(2):
                    pi = rot8.next()
                    bankmap[(s, jj, part, blk)] = pi
                    for k in range(8):
                        MM(PS[pi][:, :], slot[:, k, part * 256 + jj * 128:part * 256 + (jj + 1) * 128],
                           actA[:, k, blk * 512:(blk + 1) * 512], k == 0, k == 7, tl + [tA[k][blk]], [tps[pi]])
            for jj in range(2):
                for part in range(2):
                    y, ty2, w = info[(jj, part)]
                    bk = [bankmap.get((s, jj, part, 0)), bankmap.get((s, jj, part, 1))]
                    ACT(y[:, blk * 512:(blk + 1) * 512], PS[bk[blk]][:, :], AF.Identity,
                        [tps[bk[blk]], tvec], [ty2[blk]], scale=w[1])
                    taps_blk(y, ty2, bk, w[0], w[2], blk)
            if h == 1:
                bq = bankmap[(s, 0, 0, blk)]
                assert all(bankmap[(s, jj_, pt_, blk)] == bq + 2 * jj_ + pt_ for jj_ in range(2) for pt_ in range(2))
                assert info["ys0"] % 2 == 0
                if blk == 0:
                    seam_save(s, bq)
                else:
                    seam_apply(s, bq, info["ys0"], 2 * s, info)
            run_tail()
            if blk == 1:
                unpin(slot)
                for jj in range(2):
                    def tail(c=2 * s + jj, ya=info[(jj, 0)][0], yg=info[(jj, 1)][0], tya=info[(jj, 0)][1],
                             tyg=info[(jj, 1)][1]):
                        ACT(yg[:, 0:T], yg[:, 0:T], AF.Silu, tyg, tyg)
                        TT(actT[:, c, :], yg[:, 0:T], ya[:, 0:T], ALU.mult, tyg + tya, [tact[c][0], tact[c][1]],
                           eng="pool")
                    pending_tail.append(tail)
                if l == 0 and h == 0:
                    bg_step()
                if s >= 7 and len(pinned) < NSLOT - 1:
                    kg = len(wdown_pref.setdefault((l, cnd), []))
                    if kg < 3:
                        k0 = kg * 8
                        nk = min(8, NHC - k0)
                        wdown_pref[(l, cnd)].append(
                            wload([(w_down[l][k0 * 128:(k0 + nk) * 128, 0:512], nk, 0, 512)], pin=True))
        run_tail()
        if l == 0 and h == 0:
            bg_until(24)
            derive_in(1)
            derive(1)
            derive_cross()
        if last:
            outproj_ln(l, cnd, w_down[l], NHC, lambda kc, blk: (actT[:, kc, blk * 512:(blk + 1) * 512], tact[kc][blk]),
                       G2, LG2, LB2, None, None)
        else:
            outproj_ln(l, cnd, w_down[l], NHC, lambda kc, blk: (actT[:, kc, blk * 512:(blk + 1) * 512], tact[kc][blk]),
                       G2, LG2, LB2, LGB, LBB)

    bg_until(4)
    derive_in(0)
    for h in range(2):
        load_x(h)
        modulate_in(0, h)
        mixer_even(h)
        handoff(mix_tiles, ffn_tiles)
        ffn(0, h, last=False)
        mixer_conv(h)
        ffn(1, h, last=True)
        store_y(h)
        handoff(ffn_tiles, mix_tiles)
    S.finish()
    return nc


_CACHE = {}


def _consts():
    if "c" in _CACHE:
        return _CACHE["c"]
    bf = ml_dtypes.bfloat16
    cb = np.zeros((128, 512), np.float32)
    cb[:, 0:128] = np.eye(128, dtype=np.float32)
    cb[:, 128:256] = 1.0 / 1024.0
    n = np.arange(64)
    ang = 2.0 * np.pi * np.outer(n, n) / 64.0
    c64 = np.cos(ang) / 8.0
    s64 = np.sin(ang) / 8.0
    bdc = np.zeros((128, 128))
    bds = np.zeros((128, 128))
    for g in range(2):
        bdc[g * 64:(g + 1) * 64, g * 64:(g + 1) * 64] = c64
        bds[g * 64:(g + 1) * 64, g * 64:(g + 1) * 64] = -s64
    cb[:, 256:384] = bdc
    cb[:, 384:512] = bds
    cbf = cb.astype(bf)

    def dft(Sn):
        k = np.arange(Sn, dtype=np.int64)
        m = (np.outer(k, k) % Sn).astype(np.float64)
        a = 2.0 * np.pi * m / Sn
        return np.stack([np.cos(a), np.sin(a)]).astype(np.float32) / np.float32(math.sqrt(Sn))
    dftp = np.ascontiguousarray(dft(256))
    dfts = np.ascontiguousarray(dft(1024))
    t = np.arange(1024)
    row = (t // 64).astype(np.float32)
    col = (t % 64).astype(np.float32)
    half = 32
    inv = (1.0 / (10000.0 ** (np.arange(0, half, 2, dtype=np.float32) / half))).astype(np.float32)
    ar = row[:, None] * inv[None, :]
    ac = col[:, None] * inv[None, :]
    cosF = np.concatenate([np.cos(ar), np.cos(ar), np.cos(ac), np.cos(ac)], axis=1).astype(np.float32)
    sinF = np.concatenate([-np.sin(ar), np.sin(ar), -np.sin(ac), np.sin(ac)], axis=1).astype(np.float32)
    cosF = cosF.reshape(8, 128, 64).transpose(1, 0, 2).reshape(128, 512)
    sinF = sinF.reshape(8, 128, 64).transpose(1, 0, 2).reshape(128, 512)
    _CACHE["c"] = (cbf, dftp, dfts, cosF, sinF)
    return _CACHE["c"]


def _pp(v):
    v = np.asarray(v, np.float32)
    lead = int(np.prod(v.shape[:-1])) if v.ndim > 1 else 1
    n = v.shape[-1] // 128
    return np.ascontiguousarray(v.reshape(lead, n, 128).transpose(2, 0, 1).reshape(128, lead * n))


def kernel(x_prompt, x_sample, cache_k, cache_v, c, c_ctx, w_ada, b_ada, ln_g, ln_b,
           w_in_a, q_norm_g, k_norm_g, w_out_a, w_in_c, conv_c, w_out_c, w_up, conv_f, w_down):
    f32 = np.float32
    cbf, dftp, dfts, cosF, sinF = _consts()
    if "nc" not in _CACHE:
        _CACHE["nc"] = build_program()
    nc = _CACHE["nc"]
    x_prompt = np.asarray(x_prompt, f32)
    x_sample = np.asarray(x_sample, f32)
    cache_k = np.asarray(cache_k, f32)
    cache_v = np.asarray(cache_v, f32)
    c = np.asarray(c, f32)
    c_ctx = np.asarray(c_ctx, f32)
    shared = {
        "cbf": cbf, "dftp": dftp, "dfts": dfts,
        "w_ada": np.ascontiguousarray(np.asarray(w_ada, f32)),
        "w_in_a": np.ascontiguousarray(np.asarray(w_in_a, f32)),
        "w_out_a": np.ascontiguousarray(np.asarray(w_out_a, f32)),
        "w_in_c": np.ascontiguousarray(np.asarray(w_in_c, f32)),
        "w_out_c": np.ascontiguousarray(np.asarray(w_out_c, f32)),
        "w_up": np.ascontiguousarray(np.asarray(w_up, f32)),
        "w_down": np.ascontiguousarray(np.asarray(w_down, f32)),
    }
    bada = _pp(np.asarray(b_ada, f32))
    lng = _pp(np.asarray(ln_g, f32))
    lnb = _pp(np.asarray(ln_b, f32))
    convf = _pp(np.asarray(conv_f, f32))
    convc = _pp(np.asarray(conv_c, f32))
    gqk = np.concatenate([np.asarray(q_norm_g, f32).reshape(1, 64), np.asarray(k_norm_g, f32).reshape(1, 64)], axis=1)
    gqk = np.ascontiguousarray(np.broadcast_to(gqk, (128, 128)))
    in_maps = []
    for core in range(NCORES):
        xp = x_prompt[core * 4:(core + 1) * 4].reshape(T, D)
        xs = x_sample[core].reshape(T, D)
        xa = np.concatenate([xp, xs], axis=0)
        xTh = np.ascontiguousarray(xa.reshape(2 * T, 8, 128).transpose(2, 1, 0))
        cond2 = np.stack([c_ctx, c[core]], axis=0)
        condT = np.ascontiguousarray(cond2.reshape(2, 8, 128).transpose(2, 1, 0).reshape(128, 16))
        vecs = np.concatenate([bada, lng, lnb, convf, convc, condT, gqk, cosF, sinF], axis=1).astype(f32)
        assert vecs.shape == (128, NVEC), vecs.shape
        ck = cache_k[core, 0]
        ckT = np.ascontiguousarray(np.tile(ck.transpose(2, 1, 0), (2, 1, 1)))
        cv = np.ascontiguousarray(cache_v[core, 0].reshape(4, 128, 128).transpose(1, 0, 2))
        m = dict(shared)
        m.update({"xT": xTh, "vecs": np.ascontiguousarray(vecs), "ckT": ckT, "cv": cv})
        in_maps.append(m)
    res = run_bass_kernel_spmd(nc, in_maps, core_ids=list(range(NCORES)))
    y_prompt = np.empty((32, 256, D), f32)
    y_sample = np.empty((8, 1024, D), f32)
    nk = np.empty((32, 1, 256, 2, 64), f32)
    nv = np.empty((32, 1, 256, 2, 64), f32)
    for core in range(NCORES):
        r = res.results[core]
        yT = np.asarray(r["yT"], f32)
        ya = yT.transpose(2, 1, 0).reshape(2 * T, D)
        y_prompt[core * 4:(core + 1) * 4] = ya[:T].reshape(4, 256, D)
        y_sample[core] = ya[T:]
        nk[core * 4:(core + 1) * 4, 0] = np.asarray(r["nk"], f32).reshape(4, 256, 2, 64)
        nv[core * 4:(core + 1) * 4, 0] = np.asarray(r["nv"], f32).reshape(4, 256, 2, 64)
    return (y_prompt, y_sample, nk, nv)
```
